# Optimizing a Trainium2 kernel written in Bass

```python
import jax, jax.numpy as jnp
from jax import lax
import numpy as np

D_MODEL = 1024
BATCH = 4
SEQ = 4096
DEPTH = 4

MEM_LEN = 256
MLA_HEADS = 8
MLA_NOPE = 64
MLA_ROPE = 32
MLA_V = 64
Q_LORA = 384
KV_LORA = 256
ROPE_THETA = 10000.0
Q_BLOCK = 128
HG_HEADS = 4
HG_KDIM = 128
HG_VDIM = 128
HG_CHUNK = 64
MEM_HEADS = 4
MEM_HDIM = 128
D_FF = 2816
N_BRANCH = 3
NORM_EPS = 1e-6

MLA_QK = MLA_NOPE + MLA_ROPE
HG_K = HG_HEADS * HG_KDIM
HG_V = HG_HEADS * HG_VDIM
MEM_W = MEM_HEADS * MEM_HDIM
IN_SPLITS = (Q_LORA, KV_LORA, MLA_ROPE, HG_K, HG_K, HG_V, HG_V, MEM_W, N_BRANCH * D_MODEL)
D_IN = sum(IN_SPLITS)

kernel_name = "hybrid_mla_hgrn2_memory_macaron"


def rmsnorm(x, g):
    xf = x.astype(jnp.float32)
    y = xf * lax.rsqrt(jnp.mean(xf * xf, axis=-1, keepdims=True) + NORM_EPS)
    return (y * g.astype(jnp.float32)).astype(x.dtype)


def swiglu(x, w_in, w_out):
    a, b = jnp.split(x @ w_in, 2, axis=-1)
    return (jax.nn.silu(a) * b) @ w_out


def split_sizes(z, sizes):
    idx = [int(v) for v in np.cumsum(sizes)[:-1]]
    return jnp.split(z, idx, axis=-1)


def apply_rope(t, cos, sin):
    tf = t.astype(jnp.float32)
    t1, t2 = jnp.split(tf, 2, axis=-1)
    out = jnp.concatenate([t1 * cos - t2 * sin, t2 * cos + t1 * sin], axis=-1)
    return out.astype(t.dtype)


def causal_mla_attention(q_nope, q_rope, k_nope, k_rope, v):
    B, S, H, _ = q_nope.shape
    nb = S // Q_BLOCK
    scale = MLA_QK ** -0.5
    k_pos = jnp.arange(S)

    def to_blocks(t):
        return jnp.moveaxis(t.reshape((B, nb, Q_BLOCK) + t.shape[2:]), 1, 0)

    def block(args):
        i, qn, qr = args
        s = jnp.einsum('bqhd,bkhd->bhqk', qn, k_nope) + jnp.einsum('bqhd,bkd->bhqk', qr, k_rope)
        s = s.astype(jnp.float32) * scale
        q_pos = i * Q_BLOCK + jnp.arange(Q_BLOCK)
        s = jnp.where(q_pos[:, None] >= k_pos[None, :], s, -jnp.inf)
        p = jax.nn.softmax(s, axis=-1).astype(v.dtype)
        return jnp.einsum('bhqk,bkhd->bqhd', p, v)

    o = lax.map(block, (jnp.arange(nb), to_blocks(q_nope), to_blocks(q_rope)))
    return jnp.moveaxis(o, 0, 1).reshape(B, S, H * MLA_V)


def mla_branch(c_q, c_kv, k_r, cos, sin, q_norm_g, kv_norm_g, w_uq, w_uk, w_uv):
    B, S, _ = c_q.shape
    q = (rmsnorm(c_q, q_norm_g) @ w_uq).reshape(B, S, MLA_HEADS, MLA_QK)
    q_nope = q[..., :MLA_NOPE]
    q_rope = apply_rope(q[..., MLA_NOPE:], cos[:, :, None, :], sin[:, :, None, :])
    k_rope = apply_rope(k_r, cos, sin)
    ckv = rmsnorm(c_kv, kv_norm_g)
    k_nope = (ckv @ w_uk).reshape(B, S, MLA_HEADS, MLA_NOPE)
    v = (ckv @ w_uv).reshape(B, S, MLA_HEADS, MLA_V)
    return causal_mla_attention(q_nope, q_rope, k_nope, k_rope, v)


def chunk_gated_recurrence(q, k, v, g):
    B, S, H, K = q.shape
    V = v.shape[-1]
    n = S // HG_CHUNK
    causal = jnp.tril(jnp.ones((HG_CHUNK, HG_CHUNK), dtype=bool))

    def chunks(t):
        return t.reshape(B, n, HG_CHUNK, H, t.shape[-1]).transpose(1, 0, 3, 2, 4)

    def step(state, inp):
        qc, kc, vc, gc = inp
        G = jnp.cumsum(gc, axis=2)
        o_inter = jnp.einsum('bhck,bhkv->bhcv', qc * jnp.exp(G), state)
        diff = G[:, :, :, None, :] - G[:, :, None, :, :]
        decay = jnp.exp(jnp.where(causal[:, :, None], diff, -jnp.inf))
        A = jnp.einsum('bhtsk,bhsk->bhts', qc[:, :, :, None, :] * decay, kc)
        o_intra = jnp.einsum('bhts,bhsv->bhtv', A, vc)
        G_last = G[:, :, -1:, :]
        new_state = jnp.exp(G_last[:, :, 0, :])[..., None] * state + jnp.einsum(
            'bhck,bhcv->bhkv', kc * jnp.exp(G_last - G), vc)
        return new_state, o_inter + o_intra

    s0 = jnp.zeros((B, H, K, V), jnp.float32)
    _, o = lax.scan(step, s0, (chunks(q), chunks(k), chunks(v), chunks(g)))
    return o.transpose(1, 0, 3, 2, 4).reshape(B, S, H, V)


def hgrn2_branch(q, f_logit, i_in, gate, lb, o_norm_g):
    B, S, _ = q.shape
    dt = q.dtype
    qf = jax.nn.silu(q.astype(jnp.float32)).reshape(B, S, HG_HEADS, HG_KDIM)
    z = f_logit.astype(jnp.float32)
    g = jnp.logaddexp(jnp.log(lb), jnp.log1p(-lb) + jax.nn.log_sigmoid(z))
    k = (1.0 - lb) * jax.nn.sigmoid(-z)
    g = g.reshape(B, S, HG_HEADS, HG_KDIM)
    k = k.reshape(B, S, HG_HEADS, HG_KDIM)
    v = i_in.astype(jnp.float32).reshape(B, S, HG_HEADS, HG_VDIM)
    o = chunk_gated_recurrence(qf, k, v, g)
    o = rmsnorm(o, o_norm_g) * jax.nn.silu(gate.astype(jnp.float32).reshape(B, S, HG_HEADS, HG_VDIM))
    return o.reshape(B, S, HG_V).astype(dt)


def memory_branch(q, mem_n, w_mem_kv):
    B, S, _ = q.shape
    M = mem_n.shape[1]
    qh = q.reshape(B, S, MEM_HEADS, MEM_HDIM)
    k, v = jnp.split(mem_n @ w_mem_kv, 2, axis=-1)
    k = k.reshape(B, M, MEM_HEADS, MEM_HDIM)
    v = v.reshape(B, M, MEM_HEADS, MEM_HDIM)
    s = jnp.einsum('bqhd,bmhd->bhqm', qh, k).astype(jnp.float32) * (MEM_HDIM ** -0.5)
    p = jax.nn.softmax(s, axis=-1).astype(v.dtype)
    return jnp.einsum('bhqm,bmhd->bqhd', p, v).reshape(B, S, MEM_W)


def setup_inputs(seed: int = 0) -> dict:
    key = jax.random.key(seed)
    ks = jax.random.split(key, 32)
    f32 = jnp.float32

    def w(k, shape, fan_in):
        return jax.random.normal(k, shape, f32) * (fan_in ** -0.5)

    def gain(k, shape):
        return 1.0 + 0.02 * jax.random.normal(k, shape, f32)

    x = jax.random.normal(ks[0], (BATCH, SEQ, D_MODEL), f32)
    mem = jax.random.normal(ks[1], (BATCH, MEM_LEN, D_MODEL), f32)
    offsets = jax.random.randint(ks[2], (BATCH, 1), 0, 1024, dtype=jnp.int32)
    positions = offsets + jnp.arange(SEQ, dtype=jnp.int32)[None, :]
    return {
        "x": x,
        "mem": mem,
        "positions": positions,
        "ffn1_norm": gain(ks[3], (DEPTH, D_MODEL)),
        "w_ffn1_in": w(ks[4], (DEPTH, D_MODEL, 2 * D_FF), D_MODEL),
        "w_ffn1_out": w(ks[5], (DEPTH, D_FF, D_MODEL), D_FF),
        "mix_norm": gain(ks[6], (DEPTH, D_MODEL)),
        "w_in": w(ks[7], (DEPTH, D_MODEL, D_IN), D_MODEL),
        "q_lat_norm": gain(ks[8], (DEPTH, Q_LORA)),
        "kv_lat_norm": gain(ks[9], (DEPTH, KV_LORA)),
        "w_uq": w(ks[10], (DEPTH, Q_LORA, MLA_HEADS * MLA_QK), Q_LORA),
        "w_uk": w(ks[11], (DEPTH, KV_LORA, MLA_HEADS * MLA_NOPE), KV_LORA),
        "w_uv": w(ks[12], (DEPTH, KV_LORA, MLA_HEADS * MLA_V), KV_LORA),
        "w_o_mla": w(ks[13], (DEPTH, MLA_HEADS * MLA_V, D_MODEL), MLA_HEADS * MLA_V),
        "hg_lower_bounds": jax.random.normal(ks[14], (DEPTH, HG_K), f32),
        "hg_out_norm": gain(ks[15], (DEPTH, HG_VDIM)),
        "w_o_hg": w(ks[16], (DEPTH, HG_V, D_MODEL), HG_V),
        "mem_norm": gain(ks[17], (DEPTH, D_MODEL)),
        "w_mem_kv": w(ks[18], (DEPTH, D_MODEL, 2 * MEM_W), D_MODEL),
        "w_o_mem": w(ks[19], (DEPTH, MEM_W, D_MODEL), MEM_W),
        "w_out": w(ks[20], (DEPTH, D_MODEL, D_MODEL), D_MODEL),
        "ffn2_norm": gain(ks[21], (DEPTH, D_MODEL)),
        "w_ffn2_in": w(ks[22], (DEPTH, D_MODEL, 2 * D_FF), D_MODEL),
        "w_ffn2_out": w(ks[23], (DEPTH, D_FF, D_MODEL), D_FF),
        "final_norm": gain(ks[24], (D_MODEL,)),
    }


def reference(x, mem, positions, ffn1_norm, w_ffn1_in, w_ffn1_out, mix_norm, w_in,
              q_lat_norm, kv_lat_norm, w_uq, w_uk, w_uv, w_o_mla, hg_lower_bounds,
              hg_out_norm, w_o_hg, mem_norm, w_mem_kv, w_o_mem, w_out, ffn2_norm,
              w_ffn2_in, w_ffn2_out, final_norm):
    B, S, D = x.shape
    inv_freq = ROPE_THETA ** (-jnp.arange(0, MLA_ROPE, 2, dtype=jnp.float32) / MLA_ROPE)
    ang = positions.astype(jnp.float32)[..., None] * inv_freq
    cos, sin = jnp.cos(ang), jnp.sin(ang)
    lbs = jnp.cumsum(jax.nn.softmax(hg_lower_bounds.astype(jnp.float32), axis=0), axis=0)
    lbs = lbs - lbs[0:1]

    for l in range(DEPTH):
        x = x + 0.5 * swiglu(rmsnorm(x, ffn1_norm[l]), w_ffn1_in[l], w_ffn1_out[l])
        u = rmsnorm(x, mix_norm[l])
        c_q, c_kv, k_r, hq, hf, hi, hgate, mq, gate_logits = split_sizes(u @ w_in[l], IN_SPLITS)
        y_mla = mla_branch(c_q, c_kv, k_r, cos, sin, q_lat_norm[l], kv_lat_norm[l],
                           w_uq[l], w_uk[l], w_uv[l]) @ w_o_mla[l]
        y_hg = hgrn2_branch(hq, hf, hi, hgate, lbs[l], hg_out_norm[l]) @ w_o_hg[l]
        mem_n = rmsnorm(mem, mem_norm[l])
        y_mem = memory_branch(mq, mem_n, w_mem_kv[l]) @ w_o_mem[l]
        gates = jax.nn.sigmoid(gate_logits.reshape(B, S, N_BRANCH, D))
        merged = gates[:, :, 0] * y_mla + gates[:, :, 1] * y_hg + gates[:, :, 2] * y_mem
        x = x + merged @ w_out[l]
        x = x + 0.5 * swiglu(rmsnorm(x, ffn2_norm[l]), w_ffn2_in[l], w_ffn2_out[l])

    return rmsnorm(x, final_norm)
```

```python
import contextlib
import numpy as np
import concourse.bass as bass
import concourse.mybir as mybir
from concourse.bass_utils import run_bass_kernel_spmd

F32 = mybir.dt.float32
BF16 = mybir.dt.bfloat16
I32 = mybir.dt.int32
AF = mybir.ActivationFunctionType
ALU = mybir.AluOpType
AX = mybir.AxisListType

D = 1024
DFF = 2816
NFF = DFF // 128
DIN = 6304
TT = 512
EPS = 1e-6
O_CQ, O_CKV, O_KR, O_HQ, O_HF, O_HI, O_HG, O_MQ, O_GATE = 0, 384, 640, 672, 1184, 1696, 2208, 2720, 3232


class Dep:
    __slots__ = ("w", "r")

    def __init__(self):
        self.w = None
        self.r = {}


class Kern:
    def __init__(self, nc, es):
        self.nc = nc
        self.es = es
        self.eng = dict(pe=nc.tensor, act=nc.scalar, dve=nc.vector, pool=nc.gpsimd, sp=nc.sync)
        self.sem = {}
        self.cnt = {}
        self.seen = {e: {} for e in self.eng}
        for e in ("pe", "act", "dve", "pool"):
            self._mk((e, "c"))
        self.NDS = 8
        self.drr = {}
        for e in ("sp", "pool"):
            self.drr[e] = 0
            for j in range(self.NDS):
                self._mk((e, "d", j))
        self.nbuf = 0

    def _mk(self, key):
        self.sem[key] = self.es.enter_context(self.nc.semaphore("s_" + "_".join(str(x) for x in key)))
        self.cnt[key] = 0

    def sb(self, shape, dt, name=None):
        self.nbuf += 1
        t = self.es.enter_context(self.nc.sbuf_tensor("S_" + (name or ("sb%d" % self.nbuf)), list(shape), dt))
        return t

    def ps(self, shape, dt=F32, name=None):
        self.nbuf += 1
        t = self.es.enter_context(self.nc.psum_tensor("P_" + (name or ("ps%d" % self.nbuf)), list(shape), dt))
        return t

    def op(self, e, fn, reads=(), writes=(), inc=True, dma=False, key=None):
        eng = self.eng[e]
        if key is None:
            if dma:
                j = self.drr[e] % self.NDS
                self.drr[e] += 1
                key = (e, "d", j)
                if self.cnt[key] > 0 and self.seen[e].get(key, 0) < self.cnt[key]:
                    eng.wait_ge(self.sem[key], self.cnt[key])
                    self.seen[e][key] = self.cnt[key]
            else:
                key = (e, "c")
        elif key not in self.sem:
            self._mk(key)
        need = {}
        for b in reads:
            if b.w is not None:
                need[b.w[0]] = max(need.get(b.w[0], 0), b.w[1])
        for b in writes:
            if b.w is not None:
                need[b.w[0]] = max(need.get(b.w[0], 0), b.w[1])
            for k, v in b.r.items():
                need[k] = max(need.get(k, 0), v)
        for k, v in need.items():
            if k == ("pe", "c") and e == "pe" and not dma:
                continue
            if self.seen[e].get(k, 0) >= v:
                continue
            eng.wait_ge(self.sem[k], v)
            self.seen[e][k] = v
        ins = fn(eng)
        step = 16 if (dma and key[1] == "d") else 1
        if inc:
            ins.then_inc(self.sem[key], step)
            self.cnt[key] += step
            tok = (key, self.cnt[key])
        else:
            tok = (key, self.cnt[key] + step)
        for b in reads:
            b.r[tok[0]] = max(b.r.get(tok[0], 0), tok[1])
        for b in writes:
            b.w = tok
            b.r = {}
        return ins

    def finish(self, deps):
        eng = self.eng["sp"]
        for b in deps:
            if b.w is not None:
                k, v = b.w
                if self.seen["sp"].get(k, 0) < v:
                    eng.wait_ge(self.sem[k], v)
                    self.seen["sp"][k] = v


class Rot:
    def __init__(self, tiles):
        self.t = [(t, Dep()) for t in tiles]
        self.i = 0

    def get(self):
        r = self.t[self.i % len(self.t)]
        self.i += 1
        return r


TWO_PI_HI = 6.28125
TWO_PI_LO = float(2.0 * np.pi - 6.28125)
PI_SAFE = 3.1415925


def build(cfg):
    T = cfg["T"]
    depth = cfg["depth"]
    dtot = cfg.get("dtot", depth)
    sel = cfg.get("sel", False)
    final = cfg.get("final", True)
    NT = T // TT
    PB = T // 128
    NB = 2 * PB
    nc = bass.Bass("TRN2", target_bir_lowering=False)
    es = contextlib.ExitStack()
    k = Kern(nc, es)
    op = k.op

    def din(name, shape, dt=F32):
        return nc.dram_tensor(name, list(shape), dt, kind="ExternalInput").ap()

    xT_in = din("xT", [D, T])
    memT_in = din("memT", [D, 256])
    pos_in = din("pos", [T], I32)
    w_ffn_in = [din("w_ffn1_in", [depth, D, 2 * DFF]), din("w_ffn2_in", [depth, D, 2 * DFF])]
    w_ffn_out = [din("w_ffn1_out", [depth, DFF, D]), din("w_ffn2_out", [depth, DFF, D])]
    w_in = din("w_in", [depth, D, DIN])
    w_uq = din("w_uq", [depth, 384, 768])
    w_uk = din("w_uk", [depth, 256, 512])
    w_uv = din("w_uv", [depth, 256, 512])
    w_o = [din("w_o_mla", [depth, 512, D]), din("w_o_hg", [depth, 512, D]), din("w_o_mem", [depth, 512, D])]
    w_mem_kv = din("w_mem_kv", [depth, D, 1024])
    w_out = din("w_out", [depth, D, D])
    g_ffn = [din("g_ffn1", [128, depth, 8]), din("g_ffn2", [128, depth, 8])]
    g_mix = din("g_mix", [128, depth, 8])
    g_mem = din("g_mem", [128, depth, 8])
    g_q = din("g_q", [128, depth, 3])
    g_kv = din("g_kv", [128, depth, 2])
    g_hg = din("g_hg", [128, depth])
    hlb_in = din("hlb", [128, dtot, 4])
    lsel_in = din("lsel", [128, depth, dtot, 4])
    g_final = din("g_final", [128, 8])
    c_consts = din("c_consts", [128, 8])
    c_tri = din("c_tri", [128, 128])
    c_blk = din("c_blk", [128, 128])
    c_rst = din("c_rst", [128, TT])
    c_ident = din("c_ident", [128, 128])
    yT = nc.dram_tensor("yT", [D, T], F32, kind="ExternalOutput").ap()
    xs = nc.dram_tensor("xs", [D, T], F32, kind="Internal").ap()
    flag_in = din("flag", [128, 1])
    KTs2 = [nc.dram_tensor("KTs%d" % i, [8 * 96, T], BF16, kind="Internal").ap() for i in range(2)]
    Vs2 = [nc.dram_tensor("Vs%d" % i, [8 * 128, PB * 65], BF16, kind="Internal").ap() for i in range(2)]
    Ss = nc.dram_tensor("Ss", [4 * 128, 128], F32, kind="Internal").ap()
    KTg = [nc.dram_tensor("KTg%d" % i, [2 * 2 * 96, T], BF16, kind="Internal").ap() for i in range(4)]
    Vg = [nc.dram_tensor("Vg%d" % i, [2 * 2 * 128, PB * 65], BF16, kind="Internal").ap() for i in range(4)]
    Sg = nc.dram_tensor("Sg", [2 * 4 * 128, 128], F32, kind="Internal").ap()
    RG = [[0, 1], [2, 3], [4, 5], [6, 7]]

    xs_dep = [Dep() for _ in range(NT)]
    y_dep = [Dep() for _ in range(NT)]
    kts_dep2 = [[Dep() for _ in range(NT)] for _ in range(2)]
    vs_dep2 = [[Dep() for _ in range(NT)] for _ in range(2)]
    ss_d, sg_d = Dep(), Dep()
    kg_d = [Dep() for _ in range(4)]
    vg_d = [Dep() for _ in range(4)]

    def const(name, src, shape, dt, eng="sp"):
        t = k.sb(shape, dt, name)
        d = Dep()
        op(eng, lambda e: e.dma_start(out=t[:], in_=src), writes=[d], dma=True)
        return t, d

    mean1024, mean_dep = k.sb([128, 128], BF16, "mean1024"), Dep()
    mean384 = k.sb([128, 128], BF16, "mean384")
    mean256 = k.sb([128, 128], BF16, "mean256")
    mean128 = k.sb([128, 128], BF16, "mean128")
    ones_bf = k.sb([128, 128], BF16, "ones_bf")
    ones_f = k.sb([128, 64], F32, "ones_f")
    for t_, v_ in ((mean1024, 1.0 / 1024), (mean384, 1.0 / 384), (mean256, 1.0 / 256), (mean128, 1.0 / 128),
                   (ones_bf, 1.0), (ones_f, 1.0)):
        op("dve", lambda e: e.memset(t_[:], v_), writes=[mean_dep])
    gsb0, gd0 = const("g_ffn0", g_ffn[0], [128, depth, 8], F32)
    gsb1, gd1 = const("g_ffn1s", g_ffn[1], [128, depth, 8], F32)
    gsb = [gsb0, gsb1]
    gdep = [gd0, gd1]
    gmix, gmix_d = const("g_mixs", g_mix, [128, depth, 8], F32)
    gmem, gmem_d = const("g_mems", g_mem, [128, depth, 8], F32)
    gq, gq_d = const("g_qs", g_q, [128, depth, 3], F32)
    gkv, gkv_d = const("g_kvs", g_kv, [128, depth, 2], F32)
    ghg, ghg_d = const("g_hgs", g_hg, [128, depth], F32)
    gfin, gfin_dep = const("g_fin", g_final, [128, 8], F32)
    cst, cst_d = const("cst", c_consts, [128, 8], F32)
    tri, tri_d = const("tri", c_tri, [128, 128], BF16, eng="pool")
    blk, blk_d = const("blk", c_blk, [128, 128], BF16, eng="pool")
    rst, rst_d = const("rst", c_rst, [128, TT], F32)
    ident, ident_d = const("ident", c_ident, [128, 128], BF16, eng="pool")
    hlb, hlb_d = const("hlbs", hlb_in, [128, dtot, 4], F32)
    lsel, lsel_d = const("lsels", lsel_in, [128, depth, dtot, 4], F32)
    flag, flag_d = const("flags", flag_in, [128, 1], F32)

    lb = k.sb([128, dtot, 4], F32, "lb")
    lbtmp = k.sb([128, 8], F32, "lbtmp")
    lb_d = Dep()
    op("act", lambda e: e.activation(out=hlb[:], in_=hlb[:], func=AF.Exp), reads=[hlb_d], writes=[hlb_d])
    op("dve", lambda e: e.tensor_copy(out=lbtmp[:, 0:4], in_=hlb[:, 0, :]), reads=[hlb_d], writes=[lb_d])
    for l in range(1, dtot):
        op("dve", lambda e: e.tensor_tensor(out=lbtmp[:, 0:4], in0=lbtmp[:, 0:4], in1=hlb[:, l, :], op=ALU.add),
           reads=[hlb_d, lb_d], writes=[lb_d])
    op("dve", lambda e: e.reciprocal(out=lbtmp[:, 4:8], in_=lbtmp[:, 0:4]), reads=[lb_d], writes=[lb_d])
    op("dve", lambda e: e.memset(lb[:, 0, :], 0.0), reads=[lb_d], writes=[lb_d])
    for l in range(1, dtot):
        op("dve", lambda e: e.tensor_tensor(out=lbtmp[:, 0:4], in0=hlb[:, l, :], in1=lbtmp[:, 4:8], op=ALU.mult),
           reads=[hlb_d, lb_d], writes=[lb_d])
        op("dve", lambda e: e.tensor_tensor(out=lb[:, l, :], in0=lb[:, l - 1, :], in1=lbtmp[:, 0:4], op=ALU.add),
           reads=[lb_d], writes=[lb_d])
    lbs = k.sb([128, depth, 4], F32, "lbs")
    oml = k.sb([128, depth, 4], F32, "oml")
    for s_ in range(depth):
        op("dve", lambda e: e.tensor_tensor(out=lsel[:, s_, :, :], in0=lsel[:, s_, :, :], in1=lb[:], op=ALU.mult),
           reads=[lb_d, lsel_d], writes=[lsel_d])
        op("dve", lambda e: e.tensor_copy(out=lbs[:, s_, :], in_=lsel[:, s_, 0, :]), reads=[lsel_d], writes=[lb_d])
        for l in range(1, dtot):
            op("dve", lambda e: e.tensor_tensor(out=lbs[:, s_, :], in0=lbs[:, s_, :], in1=lsel[:, s_, l, :], op=ALU.add),
               reads=[lsel_d, lb_d], writes=[lb_d])
    op("dve", lambda e: e.tensor_scalar(out=oml[:], in0=lbs[:], scalar1=-1.0, scalar2=1.0, op0=ALU.mult, op1=ALU.add),
       reads=[lb_d], writes=[lb_d])

    xtile = k.sb([128, 8, TT], F32, "xt")
    xd = Dep()
    xn = k.sb([128, 8, TT], BF16, "xn")
    xn_dep = Dep()
    rstd = k.sb([128, TT], F32, "rstd")
    rstd_dep = Dep()
    hT = k.sb([128, NFF, TT], BF16, "hT")
    h_dep = [Dep() for _ in range(NFF)]
    hc1 = [(hT[:, c, :], h_dep[c]) for c in range(NFF)]
    f32r = Rot([k.sb([128, TT], F32, "f32r%d" % i) for i in range(8)])
    sa = f32r
    psr = Rot([k.ps([128, TT], F32, "psb%d" % i) for i in range(4)])
    psO = Rot([k.ps([128, TT], F32, "psO%d" % i) for i in range(2)])
    psM = k.ps([128, TT], F32, "psM")
    psM_d = Dep()
    psT = k.ps([128, 2 * TT], BF16, "psT")
    psT_d = Dep()
    WB = 5632
    wbuf = Rot([k.sb([128, WB], BF16, "wb%d" % i) for i in range(3)])

    def load_w(src, kc, n, eng="pool"):
        wt, wd = wbuf.get()
        v = wt[:, 0:kc * n].rearrange("p (a b) -> p a b", a=kc)
        op(eng, lambda e: e.dma_start(out=v, in_=src), writes=[wd], dma=True)
        return v, wd

    def mmacc(pt, pd, pairs, reads, m=None):
        n = len(pairs)
        for i, (lt, rh) in enumerate(pairs):
            op("pe", lambda e: e.matmul(pt, lt, rh, start=(i == 0), stop=(i == n - 1)),
               reads=reads, writes=[pd], inc=(i == n - 1))

    def rms_stats(nch, meanmat, n, sqc=None):
        if sqc is None:
            sqc = hc1
        pt, pd = psr.get()
        mmacc(pt[:, 0:n], pd, [(meanmat[:], sqc[c][0][:, 0:n]) for c in range(nch)], [sqc[c][1] for c in range(nch)] + [mean_dep])
        op("act", lambda e: e.activation(out=rstd[:, 0:n], in_=pt[:, 0:n], func=AF.Sqrt, bias=EPS),
           reads=[pd], writes=[rstd_dep])
        op("dve", lambda e: e.reciprocal(out=rstd[:, 0:n], in_=rstd[:, 0:n]), reads=[rstd_dep], writes=[rstd_dep])

    def rmsnorm(xl, g_ap_fn, gd, outl, n=TT, meanmat=None, sqc=None):
        if sqc is None:
            sqc = hc1
        nch = len(xl)
        for c, (xa, xdl) in enumerate(xl):
            op("act", lambda e: e.activation(out=sqc[c][0][:, 0:n], in_=xa, func=AF.Square), reads=xdl, writes=[sqc[c][1]])
        rms_stats(nch, meanmat if meanmat is not None else mean1024, n, sqc)
        for c, (xa, xdl) in enumerate(xl):
            oa, odl = outl[c]
            op("dve", lambda e: e.scalar_tensor_tensor(out=oa, in0=xa, scalar=g_ap_fn(c), in1=rstd[:, 0:n],
                                                       op0=ALU.mult, op1=ALU.mult),
               reads=xdl + [rstd_dep, gd], writes=odl)

    def xchunks(x, xdl):
        return [(x[:, c, :], xdl) for c in range(8)]

    def xnchunks():
        return [(xn[:, c, :], [xn_dep]) for c in range(8)]

    def ffn(l, which, tiles):
        for tl in tiles:
            rmsnorm(xchunks(tl["x"], tl["xdl"]), lambda c: gsb[which][:, l, c:c + 1], gdep[which], tl["xn"], sqc=tl["hc"])
        win = w_ffn_in[which][l].rearrange("(kc p) n -> p kc n", p=128)
        wout = w_ffn_out[which][l].rearrange("(kc p) n -> p kc n", p=128)
        groups = [(g * 512, 4) for g in range(5)] + [(2560, 2)]
        for c0, nchk in groups:
            va, wad = load_w(win[:, :, c0:c0 + nchk * 128], 8, nchk * 128)
            vb, wbd = load_w(win[:, :, DFF + c0:DFF + c0 + nchk * 128], 8, nchk * 128)
            for j in range(nchk):
                ff = c0 // 128 + j
                for tl in tiles:
                    xnl = tl["xn"]
                    xnd = [d_ for (_, dl_) in xnl for d_ in dl_]
                    pa, pad = psr.get()
                    pb, pbd = psr.get()
                    mmacc(pa[:], pad, [(va[:, c, j * 128:(j + 1) * 128], xnl[c][0]) for c in range(8)], [wad] + xnd)
                    mmacc(pb[:], pbd, [(vb[:, c, j * 128:(j + 1) * 128], xnl[c][0]) for c in range(8)], [wbd] + xnd)
                    st, sd = f32r.get()
                    op("act", lambda e: e.activation(out=st[:], in_=pa[:], func=AF.Silu), reads=[pad], writes=[sd])
                    ha, hd = tl["hc"][ff]
                    op("dve", lambda e: e.tensor_tensor(out=ha, in0=st[:], in1=pb[:], op=ALU.mult),
                       reads=[sd, pbd], writes=[hd])
        for m4 in range(2):
            v0, wd0 = load_w(wout[:, 0:11, m4 * 512:(m4 + 1) * 512], 11, 512)
            v1, wd1 = load_w(wout[:, 11:22, m4 * 512:(m4 + 1) * 512], 11, 512)
            for j in range(4):
                m = m4 * 4 + j
                for tl in tiles:
                    hc = tl["hc"]
                    po, pod = psr.get()
                    pairs = [((v0 if c < 11 else v1)[:, c % 11, j * 128:(j + 1) * 128], hc[c][0]) for c in range(NFF)]
                    mmacc(po[:], pod, pairs, [wd0, wd1] + [hc[c][1] for c in range(NFF)])
                    x = tl["x"]
                    op("dve", lambda e: e.scalar_tensor_tensor(out=x[:, m, :], in0=po[:], scalar=0.5, in1=x[:, m, :],
                                                               op0=ALU.mult, op1=ALU.add),
                       reads=[pod] + tl["xdl"], writes=tl["xdl"])

    cqn = k.sb([128, 3, TT], BF16, "cqn")
    ckvn = k.sb([128, 2, TT], BF16, "ckvn")
    cn_d = Dep()
    Qt = k.sb([128, 8, TT], BF16, "Qt")
    Kt = k.sb([128, 8, TT], BF16, "Kt")
    Vt = k.sb([128, 4, 8, 65], BF16, "Vt")
    Qt_d, Kt_d, Vt_d = Dep(), Dep(), Dep()
    op("pool", lambda e: e.memset(Vt[:], 1.0), writes=[Vt_d])
    rope = k.sb([96, 2, TT], F32, "rope")
    ropei = k.sb([96, TT], I32, "ropei")
    rope_d = Dep()
    wkr = k.sb([128, 8, 2, 96], BF16, "wkr")
    wkr_d = Dep()
    op("pool", lambda e: e.memset(wkr[:], 0.0), writes=[wkr_d])
    wuqs = k.sb([128, 3, 2, 768], BF16, "wuqs")
    wuq_d = Dep()
    op("pool", lambda e: e.memset(wuqs[:], 0.0), writes=[wuq_d])
    wukv = k.sb([128, 2, 2, 512], BF16, "wukv")
    wukv_d = Dep()
    khA = k.sb([128, 2, max(2 * T, 4096)], BF16, "khA")
    kh = Rot([khA[:, i, :] for i in range(2)])
    vh = Rot([k.sb([128, NB, 65], BF16, "vh%d" % i) for i in range(2)])
    Pt = Rot([k.sb([128, TT], BF16, "Pt%d" % i) for i in range(3)])
    ofull, ofull_d = hT[:, 8:12, :], h_dep[8:12]
    hgo, hgo_d = hT[:, 12:16, :], h_dep[12:16]
    omem, omem_d = hT[:, 16:20, :], h_dep[16:20]
    memn, memn_d = hT[:, 8:12, :].rearrange("p a (b c) -> p (a b) c", c=256), h_dep[8:12]
    mx = k.sb([128, 8, TT], BF16, "mx")
    mqT = mx[:, 0:4, :]
    mq_d = Dep()
    memK = k.sb([128, 4, 256], BF16, "memK")
    memV = k.sb([128, 2, 512], BF16, "memV")
    memKV_d = Dep()
    mergedb = k.sb([128, 8, TT], BF16, "mergedb")
    mgb_d = Dep()
    hgb = k.sb([128, TT], BF16, "hgb")
    hgb_d = Dep()
    vtok = mx[:, 4:8, :]
    vtok_d = Dep()
    qtl = k.sb([128, TT], BF16, "qtl")
    ktl = k.sb([128, TT], BF16, "ktl")
    khat = k.sb([128, TT], BF16, "khat")
    khtok = k.sb([128, 4, 128], BF16, "khtok")
    ATb = k.sb([128, 4, 128], BF16, "ATb")
    qtl_d, ktl_d, khat_d, khtok_d, ATb_d = Dep(), Dep(), Dep(), Dep(), Dep()
    S32 = k.sb([128, 4, 128], F32, "S32")
    Sbf = k.sb([128, 4, 128], BF16, "Sbf")
    S_d = [Dep() for _ in range(4)]
    Sb_d = [Dep() for _ in range(4)]

    xsv = xs.rearrange("(c p) t -> p c t", p=128)
    xinv = xT_in.rearrange("(c p) t -> p c t", p=128)
    yv = yT.rearrange("(c p) t -> p c t", p=128)
    memv = memT_in.rearrange("(c p) t -> p c t", p=128)

    def layer_setup(l):
        wi = w_in[l].rearrange("(kc p) n -> p kc n", p=128)
        op("pool", lambda e: e.dma_start(out=wkr[:, :, 0, 64:96], in_=wi[:, :, O_KR:O_KR + 32]), writes=[wkr_d], dma=True)
        op("pool", lambda e: e.dma_start(out=wkr[:, :, 1, 64:80], in_=wi[:, :, O_KR + 16:O_KR + 32]), writes=[wkr_d], dma=True)
        op("pool", lambda e: e.dma_start(out=wkr[:, :, 1, 80:96], in_=wi[:, :, O_KR:O_KR + 16]), writes=[wkr_d], dma=True)
        wq = w_uq[l].rearrange("(kc p) n -> p kc n", p=128)
        op("pool", lambda e: e.dma_start(out=wuqs[:, :, 0, :], in_=wq), writes=[wuq_d], dma=True)
        wq4 = w_uq[l].rearrange("(kc p) (h c) -> p kc h c", p=128, c=96)
        dst4 = wuqs[:, :, 1, :].rearrange("p k (h c) -> p k h c", c=96)
        for kc in range(3):
            op("pool", lambda e: e.dma_start(out=dst4[:, kc, :, 64:80], in_=wq4[:, kc, :, 80:96]), writes=[wuq_d], dma=True)
            op("pool", lambda e: e.dma_start(out=dst4[:, kc, :, 80:96], in_=wq4[:, kc, :, 64:80]), writes=[wuq_d], dma=True)
        op("pool", lambda e: e.dma_start(out=wukv[:, :, 0, :], in_=w_uk[l].rearrange("(kc p) n -> p kc n", p=128)),
           writes=[wukv_d], dma=True)
        op("pool", lambda e: e.dma_start(out=wukv[:, :, 1, :], in_=w_uv[l].rearrange("(kc p) n -> p kc n", p=128)),
           writes=[wukv_d], dma=True)
        ml = []
        for i in range(4):
            ft, fd = f32r.get()
            op("sp", lambda e: e.dma_start(out=ft[:].rearrange("p (a b) -> p a b", a=2), in_=memv[:, 2 * i:2 * i + 2, :]),
               writes=[fd], dma=True)
            ml += [(ft[:, 0:256], [fd]), (ft[:, 256:512], [fd])]
        rmsnorm(ml, lambda c: gmem[:, l, c:c + 1], gmem_d, [(memn[:, c, :], memn_d) for c in range(8)], n=256)
        wkv = w_mem_kv[l].rearrange("(kc p) n -> p kc n", p=128)
        vK, dK = load_w(wkv[:, :, 0:512], 8, 512)
        for h in range(4):
            pt, pd = psr.get()
            mmacc(pt[:, 0:256], pd, [(vK[:, c, h * 128:(h + 1) * 128], memn[:, c, :]) for c in range(8)], [dK] + memn_d)
            op("act", lambda e: e.activation(out=memK[:, h, :], in_=pt[:, 0:256], func=AF.Copy), reads=[pd], writes=[memKV_d])
        vV, dV = load_w(wkv[:, :, 512:1024], 8, 512)
        for mb in range(2):
            pt, pd = psr.get()
            mmacc(pt[:], pd, [(memn[:, c, mb * 128:(mb + 1) * 128], vV[:, c, :]) for c in range(8)], [dV] + memn_d)
            op("act", lambda e: e.activation(out=memV[:, mb, :], in_=pt[:], func=AF.Copy), reads=[pd], writes=[memKV_d])
        op("sp", lambda e: e.dma_start(out=S32[:], in_=Sg[0:512, :].rearrange("(h p) v -> p h v", p=128)),
           reads=[sg_d], writes=S_d, dma=True)
        op("dve", lambda e: e.tensor_scalar(out=S32[:], in0=S32[:], scalar1=flag[:, 0:1], scalar2=None, op0=ALU.mult),
           reads=S_d + [flag_d], writes=S_d)
        op("act", lambda e: e.activation(out=Sbf[:], in_=S32[:], func=AF.Copy), reads=S_d, writes=Sb_d)

    def rope_tables(t):
        R = slice(64, 96)
        scr = [f32r.get() for _ in range(4)]
        ang, tmp, nf, r = [b_[0][R, :] for b_ in scr]
        rd = [rope_d] + [b_[1] for b_ in scr]
        op("sp", lambda e: e.dma_start(out=ropei[R, :], in_=pos_in[t * TT:(t + 1) * TT].partition_broadcast(32)),
           writes=rd, dma=True)
        op("dve", lambda e: e.tensor_copy(out=tmp, in_=ropei[R, :]), reads=rd, writes=rd)
        op("dve", lambda e: e.tensor_scalar(out=ang, in0=tmp, scalar1=cst[R, 0:1], scalar2=None, op0=ALU.mult),
           reads=rd + [cst_d], writes=rd)
        for which in range(2):
            if which == 0:
                op("dve", lambda e: e.tensor_scalar(out=tmp, in0=ang, scalar1=float(np.pi / 2), scalar2=None, op0=ALU.add),
                   reads=rd, writes=rd)
                src = tmp
            else:
                src = ang
            op("dve", lambda e: e.tensor_scalar(out=ropei[R, :], in0=src, scalar1=float(1.0 / (2 * np.pi)), scalar2=None,
                                                op0=ALU.mult), reads=rd, writes=rd)
            op("dve", lambda e: e.tensor_copy(out=nf, in_=ropei[R, :]), reads=rd, writes=rd)
            op("dve", lambda e: e.scalar_tensor_tensor(out=r, in0=nf, scalar=-TWO_PI_HI, in1=src, op0=ALU.mult, op1=ALU.add),
               reads=rd, writes=rd)
            op("dve", lambda e: e.scalar_tensor_tensor(out=r, in0=nf, scalar=-TWO_PI_LO, in1=r, op0=ALU.mult, op1=ALU.add),
               reads=rd, writes=rd)
            op("dve", lambda e: e.tensor_scalar(out=r, in0=r, scalar1=-PI_SAFE, scalar2=PI_SAFE, op0=ALU.max, op1=ALU.min),
               reads=rd, writes=rd)
            op("act", lambda e: e.activation(out=rope[R, which, :], in_=r, func=AF.Sin), reads=rd, writes=rd)
        op("dve", lambda e: e.tensor_scalar(out=rope[R, 1, :], in0=rope[R, 1, :], scalar1=cst[R, 1:2], scalar2=None,
                                            op0=ALU.mult), reads=rd + [cst_d], writes=rd)

    def apply_rope(pA, pAd, pB, pBd, dst, dst_d):
        R = slice(64, 96)
        sc1, sc2 = f32r.get(), f32r.get()
        t1, t2 = sc1[0][R, :], sc2[0][R, :]
        op("dve", lambda e: e.tensor_tensor(out=t1, in0=pA[R, :], in1=rope[R, 0, :], op=ALU.mult),
           reads=[pAd, rope_d], writes=[sc1[1]])
        op("dve", lambda e: e.tensor_tensor(out=t2, in0=pB[R, :], in1=rope[R, 1, :], op=ALU.mult),
           reads=[pBd, rope_d], writes=[sc2[1]])
        op("dve", lambda e: e.tensor_tensor(out=dst, in0=t1, in1=t2, op=ALU.add), reads=[sc1[1], sc2[1]], writes=[dst_d])

    def mla_proj(l, t, wi):
        v, wd = load_w(wi[:, :, 0:640], 8, 640)
        cl = []
        for j in range(5):
            pt, pd = psr.get()
            mmacc(pt[:], pd, [(v[:, c, j * 128:(j + 1) * 128], xn[:, c, :]) for c in range(8)], [wd, xn_dep])
            ft, fd = f32r.get()
            op("act", lambda e: e.activation(out=ft[:], in_=pt[:], func=AF.Copy), reads=[pd], writes=[fd])
            cl.append((ft[:], [fd]))
        rmsnorm(cl[0:3], lambda c: gq[:, l, c:c + 1], gq_d, [(cqn[:, c, :], [cn_d]) for c in range(3)], meanmat=mean384)
        rmsnorm(cl[3:5], lambda c: gkv[:, l, c:c + 1], gkv_d, [(ckvn[:, c, :], [cn_d]) for c in range(2)], meanmat=mean256)
        rope_tables(t)
        pA, pAd = psr.get()
        pB, pBd = psr.get()
        mmacc(pA[0:96, :], pAd, [(wkr[:, c, 0, :], xn[:, c, :]) for c in range(8)], [wkr_d, xn_dep])
        mmacc(pB[0:96, :], pBd, [(wkr[:, c, 1, :], xn[:, c, :]) for c in range(8)], [wkr_d, xn_dep])
        apply_rope(pA, pAd, pB, pBd, Kt[64:96, 0, :], Kt_d)
        for h in range(1, 8):
            op("dve", lambda e: e.tensor_copy(out=Kt[64:96, h, :], in_=Kt[64:96, 0, :]), reads=[Kt_d], writes=[Kt_d])
        for h in range(8):
            pA, pAd = psr.get()
            pB, pBd = psr.get()
            mmacc(pA[0:96, :], pAd, [(wuqs[:, c, 0, h * 96:(h + 1) * 96], cqn[:, c, :]) for c in range(3)], [wuq_d, cn_d])
            mmacc(pB[0:96, :], pBd, [(wuqs[:, c, 1, h * 96:(h + 1) * 96], cqn[:, c, :]) for c in range(3)], [wuq_d, cn_d])
            op("act", lambda e: e.activation(out=Qt[0:64, h, :], in_=pA[0:64, :], func=AF.Copy), reads=[pAd], writes=[Qt_d])
            apply_rope(pA, pAd, pB, pBd, Qt[64:96, h, :], Qt_d)
        for hp in range(4):
            pt, pd = psr.get()
            mmacc(pt[:], pd, [(wukv[:, c, 0, hp * 128:(hp + 1) * 128], ckvn[:, c, :]) for c in range(2)], [wukv_d, cn_d])
            op("act", lambda e: e.activation(out=Kt[0:64, 2 * hp, :], in_=pt[0:64, :], func=AF.Copy), reads=[pd], writes=[Kt_d])
            op("act", lambda e: e.activation(out=Kt[0:64, 2 * hp + 1, :], in_=pt[64:128, :], func=AF.Copy), reads=[pd], writes=[Kt_d])
        for tb in range(4):
            pt, pd = psr.get()
            mmacc(pt[:], pd, [(ckvn[:, c, tb * 128:(tb + 1) * 128], wukv[:, c, 1, :]) for c in range(2)], [wukv_d, cn_d])
            op("act", lambda e: e.activation(out=Vt[:, tb, :, 0:64], in_=pt[:].rearrange("p (h c) -> p h c", c=64),
                                             func=AF.Copy), reads=[pd], writes=[Vt_d])
        par = l % 2
        op("sp", lambda e: e.dma_start(out=KTs2[par].rearrange("(h p) t -> p h t", p=96)[:, :, t * TT:(t + 1) * TT], in_=Kt[0:96, :, :]),
           reads=[Kt_d], writes=[kts_dep2[par][t]], dma=True)
        vs4 = Vs2[par].rearrange("(h p) (b c) -> h p b c", p=128, c=65)
        for h in range(8):
            op("sp", lambda e: e.dma_start(out=vs4[h, :, t * 4:(t + 1) * 4, :], in_=Vt[:, :, h, :]),
               reads=[Vt_d], writes=[vs_dep2[par][t]], dma=True)

    SCALE = float(96 ** -0.5)

    def mla_attn(l, t):
        par = l % 2
        nkb = PB + 4 * (t + 1)
        kts4 = KTs2[par].rearrange("(h p) t -> h p t", p=96)
        vs4 = Vs2[par].rearrange("(h p) (b c) -> h p b c", p=128, c=65)
        ktg4 = [g_.rearrange("(r h p) t -> r h p t", r=2, p=96) for g_ in KTg]
        vg5 = [g_.rearrange("(r h p) (b c) -> r h p b c", r=2, p=128, c=65) for g_ in Vg]
        for h in range(8):
            kt, kd = kh.get()
            vt, vd = vh.get()
            op("sp", lambda e: e.dma_start(out=kt[0:96, 0:T], in_=ktg4[h // 2][0, h % 2, :, :]), reads=[kg_d[h // 2]], writes=[kd], dma=True)
            op("sp", lambda e: e.dma_start(out=kt[0:96, T:T + (t + 1) * TT], in_=kts4[h, :, 0:(t + 1) * TT]),
               reads=kts_dep2[par][0:t + 1], writes=[kd], dma=True)
            op("sp", lambda e: e.dma_start(out=vt[:, 0:PB, :], in_=vg5[h // 2][0, h % 2, :, :, :]), reads=[vg_d[h // 2]], writes=[vd], dma=True)
            op("sp", lambda e: e.dma_start(out=vt[:, PB:nkb, :], in_=vs4[h, :, 0:4 * (t + 1), :]),
               reads=vs_dep2[par][0:t + 1], writes=[vd], dma=True)
            op("dve", lambda e: e.tensor_scalar(out=vt[:, 0:PB, :], in0=vt[:, 0:PB, :], scalar1=flag[:, 0:1], scalar2=None,
                                                op0=ALU.mult), reads=[vd, flag_d], writes=[vd])
            po, pod = psO.get()

            def q0_of(kb):
                return max(0, kb - PB - 4 * t) * 128

            def emit_S(kb):
                ps_, psd = psr.get()
                q0 = q0_of(kb)
                op("pe", lambda e: e.matmul(ps_[:, q0:TT], kt[0:96, kb * 128:(kb + 1) * 128], Qt[0:96, h, q0:TT],
                                            start=True, stop=True), reads=[kd, Qt_d], writes=[psd])
                return ps_, psd

            nxt = emit_S(0)
            for kb in range(nkb):
                ps_, psd = nxt
                if kb + 1 < nkb:
                    nxt = emit_S(kb + 1)
                q0 = q0_of(kb)
                p_, p_d = Pt.get()
                op("act", lambda e: e.activation(out=p_[:, q0:TT], in_=ps_[:, q0:TT], func=AF.Exp, scale=SCALE),
                   reads=[psd], writes=[p_d])
                if kb >= PB + 4 * t:
                    op("dve", lambda e: e.tensor_tensor(out=p_[:, q0:q0 + 128], in0=p_[:, q0:q0 + 128], in1=tri[:],
                                                         op=ALU.mult), reads=[p_d, tri_d], writes=[p_d])
                op("pe", lambda e: e.matmul(po[0:65, q0:TT], vt[:, kb, :], p_[:, q0:TT], start=(kb == 0),
                                            stop=(kb == nkb - 1)), reads=[vd, p_d], writes=[pod], inc=True)
            rrow, rrow_d = f32r.get()
            o32, o32_d = f32r.get()
            op("dve", lambda e: e.reciprocal(out=rrow[64:65, :], in_=po[64:65, :]), reads=[pod], writes=[rrow_d])
            op("pe", lambda e: e.matmul(psM[0:64, :], ones_f[64:65, :], rrow[64:65, :], start=True, stop=True),
               reads=[rrow_d, mean_dep], writes=[psM_d])
            op("act", lambda e: e.activation(out=o32[0:64, :], in_=po[0:64, :], func=AF.Copy), reads=[pod], writes=[o32_d])
            r0 = (h % 2) * 64
            op("dve", lambda e: e.tensor_tensor(out=ofull[r0:r0 + 64, h // 2, :], in0=o32[0:64, :], in1=psM[0:64, :],
                                                op=ALU.mult), reads=[o32_d, psM_d], writes=[ofull_d[h // 2]])

    MSCALE = float(128 ** -0.5)

    def mem_attn(l, wi):
        v, wd = load_w(wi[:, :, O_MQ:O_MQ + 512], 8, 512)
        for h in range(4):
            pt, pd = psr.get()
            mmacc(pt[:], pd, [(v[:, c, h * 128:(h + 1) * 128], xn[:, c, :]) for c in range(8)], [wd, xn_dep])
            op("act", lambda e: e.activation(out=mqT[:, h, :], in_=pt[:], func=AF.Copy), reads=[pd], writes=[mq_d])
        for h in range(4):
            po, pod = psO.get()
            pden, pdend = psO.get()
            ps_list = []
            for mb in range(2):
                ps_, psd = psr.get()
                op("pe", lambda e: e.matmul(ps_[:], memK[:, h, mb * 128:(mb + 1) * 128], mqT[:, h, :], start=True, stop=True),
                   reads=[memKV_d, mq_d], writes=[psd])
                p_, p_d = Pt.get()
                op("act", lambda e: e.activation(out=p_[:], in_=ps_[:], func=AF.Exp, scale=MSCALE), reads=[psd], writes=[p_d])
                ps_list.append((p_, p_d))
            for mb in range(2):
                p_, p_d = ps_list[mb]
                op("pe", lambda e: e.matmul(po[:], memV[:, mb, h * 128:(h + 1) * 128], p_[:], start=(mb == 0), stop=(mb == 1)),
                   reads=[memKV_d, p_d], writes=[pod])
            for mb in range(2):
                p_, p_d = ps_list[mb]
                op("pe", lambda e: e.matmul(pden[:], ones_bf[:], p_[:], start=(mb == 0), stop=(mb == 1)),
                   reads=[mean_dep, p_d], writes=[pdend])
            st, sd = f32r.get()
            op("dve", lambda e: e.reciprocal(out=st[:], in_=pden[:]), reads=[pdend], writes=[sd])
            op("dve", lambda e: e.tensor_tensor(out=omem[:, h, :], in0=st[:], in1=po[:], op=ALU.mult),
               reads=[sd, pod], writes=[omem_d[h]])

    def hgrn(l, t, wi):
        v, wd = load_w(wi[:, :, O_HI:O_HI + 512], 8, 512)
        for tb in range(4):
            pt, pd = psr.get()
            mmacc(pt[:], pd, [(xn[:, c, tb * 128:(tb + 1) * 128], v[:, c, :]) for c in range(8)], [wd, xn_dep])
            op("act", lambda e: e.activation(out=vtok[:, tb, :], in_=pt[:], func=AF.Copy), reads=[pd], writes=[vtok_d])
        for h in range(4):
            wt, wd = wbuf.get()
            v = wt[:, 0:8 * 384].rearrange("p (a b) -> p a b", a=8)
            for i, off in enumerate((O_HQ, O_HF, O_HG)):
                op("pool", lambda e: e.dma_start(out=v[:, :, i * 128:(i + 1) * 128],
                                                 in_=wi[:, :, off + h * 128:off + (h + 1) * 128]), writes=[wd], dma=True)
            q_, q_d = f32r.get()
            f_, f_d = f32r.get()
            g_, g_d = f32r.get()
            G_, G_d = f32r.get()
            E_, E_d = f32r.get()
            for i, (dst, dd, fn) in enumerate(((q_, q_d, AF.Silu), (f_, f_d, AF.Sigmoid), (hgb, hgb_d, AF.Silu))):
                pt, pd = psr.get()
                mmacc(pt[:], pd, [(v[:, c, i * 128:(i + 1) * 128], xn[:, c, :]) for c in range(8)], [wd, xn_dep])
                op("act", lambda e: e.activation(out=dst[:], in_=pt[:], func=fn), reads=[pd], writes=[dd])
            op("dve", lambda e: e.tensor_scalar(out=f_[:], in0=f_[:], scalar1=oml[:, l, h:h + 1], scalar2=lbs[:, l, h:h + 1],
                                                op0=ALU.mult, op1=ALU.add), reads=[f_d, lb_d], writes=[f_d])
            op("act", lambda e: e.activation(out=g_[:], in_=f_[:], func=AF.Ln), reads=[f_d], writes=[g_d])
            op("dve", lambda e: e.tensor_tensor_scan(out=G_[:], data0=rst[:], data1=g_[:], initial=0.0, op0=ALU.mult, op1=ALU.add),
               reads=[g_d, rst_d], writes=[G_d])
            op("act", lambda e: e.activation(out=E_[:], in_=G_[:], func=AF.Exp), reads=[G_d], writes=[E_d])
            op("dve", lambda e: e.tensor_tensor(out=qtl[:], in0=q_[:], in1=E_[:], op=ALU.mult),
               reads=[q_d, E_d], writes=[qtl_d])
            op("act", lambda e: e.activation(out=g_[:], in_=G_[:], func=AF.Exp, scale=-1.0), reads=[G_d, g_d], writes=[g_d])
            op("dve", lambda e: e.tensor_scalar(out=f_[:], in0=f_[:], scalar1=-1.0, scalar2=1.0, op0=ALU.mult, op1=ALU.add),
               reads=[f_d], writes=[f_d])
            k32, k32_d = q_, q_d
            op("dve", lambda e: e.tensor_tensor(out=k32[:], in0=f_[:], in1=g_[:], op=ALU.mult), reads=[f_d, g_d], writes=[k32_d])
            op("act", lambda e: e.activation(out=ktl[:], in_=k32[:], func=AF.Copy), reads=[k32_d], writes=[ktl_d])
            for ci in range(8):
                cs = slice(ci * 64, (ci + 1) * 64)
                op("dve", lambda e: e.tensor_scalar(out=khat[:, cs], in0=k32[:, cs], scalar1=E_[:, ci * 64 + 63:ci * 64 + 64],
                                                    scalar2=None, op0=ALU.mult), reads=[k32_d, E_d], writes=[khat_d])
            for tb in range(4):
                op("pe", lambda e: e.transpose(psT[:, tb * 128:(tb + 1) * 128], khat[:, tb * 128:(tb + 1) * 128], ident[:]),
                   reads=[khat_d, ident_d], writes=[psT_d], inc=(tb == 3))
            op("act", lambda e: e.activation(out=khtok[:].rearrange("p a b -> p (a b)"), in_=psT[:, 0:512], func=AF.Copy),
               reads=[psT_d], writes=[khtok_d])
            pat, patd = psr.get()
            for tb in range(4):
                bs = slice(tb * 128, (tb + 1) * 128)
                op("pe", lambda e: e.matmul(pat[:, bs], ktl[:, bs], qtl[:, bs], start=True, stop=True),
                   reads=[ktl_d, qtl_d], writes=[patd], inc=(tb == 3))
            for tb in range(4):
                bs = slice(tb * 128, (tb + 1) * 128)
                op("dve", lambda e: e.tensor_tensor(out=ATb[:, tb, :], in0=pat[:, bs], in1=blk[:], op=ALU.mult),
                   reads=[patd, blk_d], writes=[ATb_d])
            po, pod = psO.get()
            for tb in range(4):
                bs = slice(tb * 128, (tb + 1) * 128)
                op("pe", lambda e: e.matmul(po[:, bs], vtok[:, tb, h * 128:(h + 1) * 128], ATb[:, tb, :], start=True, stop=False),
                   reads=[vtok_d, ATb_d], writes=[pod], inc=False)
                for half in range(2):
                    ci = tb * 2 + half
                    cs = slice(ci * 64, (ci + 1) * 64)
                    rs = slice(half * 64, half * 64 + 64)
                    last = (half == 1)
                    op("pe", lambda e: e.matmul(po[:, cs], Sbf[:, h, :], qtl[:, cs], start=False, stop=last),
                       reads=[Sb_d[h], qtl_d], writes=[pod], inc=True)
                    pu, pud = psr.get()
                    op("pe", lambda e: e.matmul(pu[:, 0:128], khtok[rs, tb, :], vtok[rs, tb, h * 128:(h + 1) * 128],
                                                start=True, stop=True), reads=[khtok_d, vtok_d], writes=[pud])
                    op("dve", lambda e: e.scalar_tensor_tensor(out=S32[:, h, :], in0=S32[:, h, :],
                                                               scalar=E_[:, ci * 64 + 63:ci * 64 + 64], in1=pu[:, 0:128],
                                                               op0=ALU.mult, op1=ALU.add),
                       reads=[S_d[h], E_d, pud], writes=[S_d[h]])
                    op("act", lambda e: e.activation(out=Sbf[:, h, :], in_=S32[:, h, :], func=AF.Copy), reads=[S_d[h]], writes=[Sb_d[h]])
            o32, o32_d = f32r.get()
            st, sd = f32r.get()
            op("act", lambda e: e.activation(out=o32[:], in_=po[:], func=AF.Copy), reads=[pod], writes=[o32_d])
            op("act", lambda e: e.activation(out=hT[:, 0, :], in_=po[:], func=AF.Square), reads=[pod], writes=[h_dep[0]])
            rms_stats(1, mean128, TT)
            op("dve", lambda e: e.scalar_tensor_tensor(out=st[:], in0=o32[:], scalar=ghg[:, l:l + 1], in1=rstd[:],
                                                       op0=ALU.mult, op1=ALU.mult), reads=[o32_d, rstd_dep, ghg_d], writes=[sd])
            op("dve", lambda e: e.tensor_tensor(out=hgo[:, h, :], in0=st[:], in1=hgb[:], op=ALU.mult),
               reads=[sd, hgb_d], writes=[hgo_d[h]])

    def merge_out(l, x, xdep, wi):
        branches = ((0, ofull, ofull_d), (1, hgo, hgo_d), (2, omem, omem_d))
        for b, ob, obd in branches:
            if not cfg.get("br%d" % b, True):
                continue
            wo_v = w_o[b][l].rearrange("(kc p) n -> p kc n", p=128)
            for half in range(2):
                vo, wod = load_w(wo_v[:, :, half * 512:(half + 1) * 512], 4, 512)
                vg, wgd = load_w(wi[:, :, O_GATE + b * 1024 + half * 512:O_GATE + b * 1024 + (half + 1) * 512], 8, 512)
                for j in range(4):
                    m = half * 4 + j
                    py, pyd = psr.get()
                    pg, pgd = psr.get()
                    mmacc(py[:], pyd, [(vo[:, c, j * 128:(j + 1) * 128], ob[:, c, :]) for c in range(4)], [wod] + obd)
                    mmacc(pg[:], pgd, [(vg[:, c, j * 128:(j + 1) * 128], xn[:, c, :]) for c in range(8)], [wgd, xn_dep])
                    st, sd = f32r.get()
                    op("act", lambda e: e.activation(out=st[:], in_=pg[:], func=AF.Sigmoid), reads=[pgd], writes=[sd])
                    if b == 0:
                        op("dve", lambda e: e.tensor_tensor(out=mergedb[:, m, :], in0=st[:], in1=py[:], op=ALU.mult),
                           reads=[sd, pyd], writes=[mgb_d])
                    else:
                        op("dve", lambda e: e.tensor_tensor(out=st[:], in0=st[:], in1=py[:], op=ALU.mult),
                           reads=[sd, pyd], writes=[sd])
                        op("dve", lambda e: e.tensor_tensor(out=mergedb[:, m, :], in0=mergedb[:, m, :], in1=st[:], op=ALU.add),
                           reads=[sd, mgb_d], writes=[mgb_d])
        wov = w_out[l].rearrange("(kc p) n -> p kc n", p=128)
        for half in range(2):
            v, wd = load_w(wov[:, :, half * 512:(half + 1) * 512], 8, 512)
            for j in range(4):
                m = half * 4 + j
                pt, pd = psr.get()
                mmacc(pt[:], pd, [(v[:, c, j * 128:(j + 1) * 128], mergedb[:, c, :]) for c in range(8)], [wd, mgb_d])
                op("dve", lambda e: e.tensor_tensor(out=x[:, m, :], in0=x[:, m, :], in1=pt[:], op=ALU.add),
                   reads=[pd, xdep], writes=[xdep])

    def mixer(l, t, x, xdep):
        rmsnorm(xchunks(x, [xdep]), lambda c: gmix[:, l, c:c + 1], gmix_d, xnchunks())
        wi = w_in[l].rearrange("(kc p) n -> p kc n", p=128)
        if cfg.get("br0", True):
            mla_proj(l, t, wi)
            mla_attn(l, t)
        if cfg.get("br1", True):
            hgrn(l, t, wi)
        if cfg.get("br2", True):
            mem_attn(l, wi)
        merge_out(l, x, xdep, wi)

    zt, zd = hT[:, 0:4, :].rearrange("p a b -> p (a b)"), h_dep[0:4]
    op("dve", lambda e: e.memset(zt, 0.0), writes=zd)
    zf, zfd = f32r.get()
    op("dve", lambda e: e.memset(zf[:], 0.0), writes=[zfd])
    for pc in range(4):
        for i in range(3):
            op("sp", lambda e: e.dma_start(out=KTg[pc][i * 128:(i + 1) * 128, :], in_=zt[:, 0:T]), reads=zd, writes=[kg_d[pc]], dma=True)
        for i in range(4):
            op("sp", lambda e: e.dma_start(out=Vg[pc][i * 128:(i + 1) * 128, :], in_=zt[:, 0:PB * 65]), reads=zd, writes=[vg_d[pc]], dma=True)
    for i in range(8):
        op("sp", lambda e: e.dma_start(out=Sg[i * 128:(i + 1) * 128, :], in_=zf[:, 0:128]), reads=[zfd], writes=[sg_d], dma=True)

    def exchange(l):
        par = l % 2
        op("sp", lambda e: e.dma_start(out=Ss.rearrange("(h p) v -> p h v", p=128), in_=S32[:]), reads=S_d, writes=[ss_d], dma=True)
        for pc in range(4):
            op("pool", lambda e: e.collective_compute("AllGather", ALU.bypass, replica_groups=RG,
                                                      ins=[KTs2[par][pc * 192:(pc + 1) * 192, :]], outs=[KTg[pc]]),
               reads=kts_dep2[par], writes=[kg_d[pc]], dma=True, key=("ccK%d" % pc, "c"))
            op("pool", lambda e: e.collective_compute("AllGather", ALU.bypass, replica_groups=RG,
                                                      ins=[Vs2[par][pc * 256:(pc + 1) * 256, :]], outs=[Vg[pc]]),
               reads=vs_dep2[par], writes=[vg_d[pc]], dma=True, key=("ccV%d" % pc, "c"))
        op("pool", lambda e: e.collective_compute("AllGather", ALU.bypass, replica_groups=RG, ins=[Ss],
                                                  outs=[Sg]),
           reads=[ss_d], writes=[sg_d], dma=True, key=("ccS", "c"))

    khdeps = [kh.t[0][1], kh.t[1][1]]
    x_b = khA[:].rearrange("p a b -> p (a b)").bitcast(F32).rearrange("p (c t) -> p c t", c=8)
    xn_b = [(mx[:, c, :], [mq_d if c < 4 else vtok_d]) for c in range(8)]
    hc2 = [(Qt[:, c, :], Qt_d) for c in range(8)] + [(Kt[:, c, :], Kt_d) for c in range(8)] + \
          [(mergedb[:, c, :], mgb_d) for c in range(6)]
    tile_a = dict(x=xtile[:], xdl=[xd], xn=xnchunks(), hc=hc1)
    tile_b = dict(x=x_b, xdl=khdeps, xn=xn_b, hc=hc2)

    def ffn_phase(l_prev, l_next, final_out):
        for p in range(NT // 2):
            tls = [tile_a, tile_b]
            for i, tl in enumerate(tls):
                t = 2 * p + i
                src = xinv if (l_prev is None) else xsv
                op("sp", lambda e: e.dma_start(out=tl["x"], in_=src[:, :, t * TT:(t + 1) * TT]),
                   reads=[xs_dep[t]], writes=tl["xdl"], dma=True)
            if l_prev is not None:
                ffn(l_prev, 1, tls)
            if l_next is not None:
                ffn(l_next, 0, tls)
            for i, tl in enumerate(tls):
                t = 2 * p + i
                if final_out:
                    rmsnorm(xchunks(tl["x"], tl["xdl"]), lambda c: gfin[:, c:c + 1], gfin_dep, xchunks(tl["x"], tl["xdl"]),
                            sqc=tl["hc"])
                    op("sp", lambda e: e.dma_start(out=yv[:, :, t * TT:(t + 1) * TT], in_=tl["x"]),
                       reads=tl["xdl"], writes=[y_dep[t]], dma=True)
                else:
                    op("sp", lambda e: e.dma_start(out=xsv[:, :, t * TT:(t + 1) * TT], in_=tl["x"]),
                       reads=tl["xdl"], writes=[xs_dep[t]], dma=True)

    for l in range(depth):
        ffn_phase(None if l == 0 else l - 1, l, False)
        layer_setup(l)
        for t in range(NT):
            x = xtile
            op("sp", lambda e: e.dma_start(out=x[:], in_=xsv[:, :, t * TT:(t + 1) * TT]),
               reads=[xs_dep[t]], writes=[xd], dma=True)
            mixer(l, t, x, xd)
            op("sp", lambda e: e.dma_start(out=xsv[:, :, t * TT:(t + 1) * TT], in_=x[:]),
               reads=[xd], writes=[xs_dep[t]], dma=True)
        if l < depth - 1:
            exchange(l)
    ffn_phase(depth - 1, None, True)
    k.finish(y_dep)
    return nc, es


def host_inputs(inp, b, half, zc, depth=4, TL=2048):
    f = np.float32
    NS = depth + 1

    def stack(a):
        a = np.asarray(a, f)
        key = (id(a), half)
        z = np.zeros((1,) + a.shape[1:], f)
        return np.concatenate([a, z], 0) if half == 0 else np.concatenate([z, a], 0)

    def fm(g, n):
        return np.ascontiguousarray(stack(g).reshape(NS, n, 128).transpose(2, 0, 1))

    consts = np.zeros((128, 8), f)
    inv_freq = (np.float32(10000.0) ** (-np.arange(0, 32, 2, dtype=np.float32) / np.float32(32))).astype(f)
    consts[64:80, 0] = inv_freq
    consts[80:96, 0] = inv_freq
    consts[64:80, 1] = -1.0
    consts[80:96, 1] = 1.0
    idx = np.arange(128)
    tri = (idx[None, :] >= idx[:, None]).astype(f)
    blk = tri * ((idx[None, :] // 64) == (idx[:, None] // 64))
    rst = np.ones((128, TT), f)
    rst[:, ::64] = 0.0
    onehot = np.zeros((128, NS, depth, 4), f)
    for l in range(depth):
        onehot[:, l + half, l, :] = 1.0
    sl = slice(half * TL, (half + 1) * TL)
    m = dict(
        xT=np.ascontiguousarray(np.asarray(inp["x"][b], f)[sl].T),
        memT=np.ascontiguousarray(np.asarray(inp["mem"][b], f).T),
        pos=np.ascontiguousarray(np.asarray(inp["positions"][b], np.int32)[sl]),
        flag=np.full((128, 1), float(half), f),
        g_ffn1=fm(inp["ffn1_norm"], 8), g_ffn2=fm(inp["ffn2_norm"], 8), g_mix=fm(inp["mix_norm"], 8),
        g_mem=fm(inp["mem_norm"], 8), g_q=fm(inp["q_lat_norm"], 3), g_kv=fm(inp["kv_lat_norm"], 2),
        g_hg=np.ascontiguousarray(stack(inp["hg_out_norm"]).T),
        hlb=np.ascontiguousarray(np.asarray(inp["hg_lower_bounds"], f).reshape(depth, 4, 128).transpose(2, 0, 1)),
        lsel=onehot,
        g_final=np.ascontiguousarray(np.asarray(inp["final_norm"], f).reshape(8, 128).T),
        c_consts=consts, c_tri=tri, c_blk=blk.astype(f), c_rst=rst, c_ident=np.eye(128, dtype=f),
    )
    for nm in ("w_ffn1_in", "w_ffn1_out", "w_ffn2_in", "w_ffn2_out", "w_in", "w_uq", "w_uk", "w_uv", "w_o_mla",
               "w_o_hg", "w_o_mem", "w_mem_kv", "w_out"):
        if (nm, half) not in zc:
            zc[(nm, half)] = stack(inp[nm])
        m[nm] = zc[(nm, half)]
    return m


def kernel(**inp):
    B, S, depth = 4, 4096, 4
    TL = S // 2
    nc, es = build(dict(T=TL, depth=depth + 1, dtot=depth))
    zc = {}
    maps = [host_inputs(inp, core // 2, core % 2, zc) for core in range(8)]
    res = run_bass_kernel_spmd(nc, maps, core_ids=list(range(8)))
    out = np.empty((B, S, D), np.float32)
    for core in range(8):
        b, half = core // 2, core % 2
        out[b, half * TL:(half + 1) * TL, :] = res.results[core]["yT"].T
    return out
```

```python
import contextlib
import numpy as np
import concourse.bass as bass
import concourse.mybir as mybir
from concourse.bass_utils import run_bass_kernel_spmd

F32 = mybir.dt.float32
BF16 = mybir.dt.bfloat16
I32 = mybir.dt.int32
AF = mybir.ActivationFunctionType
ALU = mybir.AluOpType
AX = mybir.AxisListType

D = 1024
DFF = 2816
NFF = DFF // 128
DIN = 6304
TT = 512
EPS = 1e-6
O_CQ, O_CKV, O_KR, O_HQ, O_HF, O_HI, O_HG, O_MQ, O_GATE = 0, 384, 640, 672, 1184, 1696, 2208, 2720, 3232


class Dep:
    __slots__ = ("w", "r")

    def __init__(self):
        self.w = None
        self.r = {}


class Kern:
    def __init__(self, nc, es):
        self.nc = nc
        self.es = es
        self.eng = dict(pe=nc.tensor, act=nc.scalar, dve=nc.vector, pool=nc.gpsimd, sp=nc.sync)
        self.sem = {}
        self.cnt = {}
        self.seen = {e: {} for e in self.eng}
        for e in ("pe", "act", "dve", "pool"):
            self._mk((e, "c"))
        self.NDS = 8
        self.drr = {}
        for e in ("sp", "pool"):
            self.drr[e] = 0
            for j in range(self.NDS):
                self._mk((e, "d", j))
        self.nbuf = 0

    def _mk(self, key):
        self.sem[key] = self.es.enter_context(self.nc.semaphore("s_" + "_".join(str(x) for x in key)))
        self.cnt[key] = 0

    def sb(self, shape, dt, name=None):
        self.nbuf += 1
        t = self.es.enter_context(self.nc.sbuf_tensor("S_" + (name or ("sb%d" % self.nbuf)), list(shape), dt))
        return t

    def ps(self, shape, dt=F32, name=None):
        self.nbuf += 1
        t = self.es.enter_context(self.nc.psum_tensor("P_" + (name or ("ps%d" % self.nbuf)), list(shape), dt))
        return t

    def op(self, e, fn, reads=(), writes=(), inc=True, dma=False, key=None):
        eng = self.eng[e]
        if key is None:
            if dma:
                j = self.drr[e] % self.NDS
                self.drr[e] += 1
                key = (e, "d", j)
                if self.cnt[key] > 0 and self.seen[e].get(key, 0) < self.cnt[key]:
                    eng.wait_ge(self.sem[key], self.cnt[key])
                    self.seen[e][key] = self.cnt[key]
            else:
                key = (e, "c")
        elif key not in self.sem:
            self._mk(key)
        need = {}
        for b in reads:
            if b.w is not None:
                need[b.w[0]] = max(need.get(b.w[0], 0), b.w[1])
        for b in writes:
            if b.w is not None:
                need[b.w[0]] = max(need.get(b.w[0], 0), b.w[1])
            for k, v in b.r.items():
                need[k] = max(need.get(k, 0), v)
        for k, v in need.items():
            if k == ("pe", "c") and e == "pe" and not dma:
                continue
            if self.seen[e].get(k, 0) >= v:
                continue
            eng.wait_ge(self.sem[k], v)
            self.seen[e][k] = v
        ins = fn(eng)
        step = 16 if (dma and key[1] == "d") else 1
        if inc:
            ins.then_inc(self.sem[key], step)
            self.cnt[key] += step
            tok = (key, self.cnt[key])
        else:
            tok = (key, self.cnt[key] + step)
        for b in reads:
            b.r[tok[0]] = max(b.r.get(tok[0], 0), tok[1])
        for b in writes:
            b.w = tok
            b.r = {}
        return ins

    def finish(self, deps):
        eng = self.eng["sp"]
        for b in deps:
            if b.w is not None:
                k, v = b.w
                if self.seen["sp"].get(k, 0) < v:
                    eng.wait_ge(self.sem[k], v)
                    self.seen["sp"][k] = v


class Rot:
    def __init__(self, tiles):
        self.t = [(t, Dep()) for t in tiles]
        self.i = 0

    def get(self):
        r = self.t[self.i % len(self.t)]
        self.i += 1
        return r


TWO_PI_HI = 6.28125
TWO_PI_LO = float(2.0 * np.pi - 6.28125)
PI_SAFE = 3.1415925


def build(cfg):
    T = cfg["T"]
    depth = cfg["depth"]
    dtot = cfg.get("dtot", depth)
    sel = cfg.get("sel", False)
    final = cfg.get("final", True)
    NT = T // TT
    PB = T // 128
    NB = 2 * PB
    nc = bass.Bass("TRN2", target_bir_lowering=False)
    es = contextlib.ExitStack()
    k = Kern(nc, es)
    op = k.op

    def din(name, shape, dt=F32):
        return nc.dram_tensor(name, list(shape), dt, kind="ExternalInput").ap()

    xT_in = din("xT", [D, T])
    memT_in = din("memT", [D, 256])
    pos_in = din("pos", [T], I32)
    w_ffn_in = [din("w_ffn1_in", [depth, D, 2 * DFF]), din("w_ffn2_in", [depth, D, 2 * DFF])]
    w_ffn_out = [din("w_ffn1_out", [depth, DFF, D]), din("w_ffn2_out", [depth, DFF, D])]
    w_in = din("w_in", [depth, D, DIN])
    w_uq = din("w_uq", [depth, 384, 768])
    w_uk = din("w_uk", [depth, 256, 512])
    w_uv = din("w_uv", [depth, 256, 512])
    w_o = [din("w_o_mla", [depth, 512, D]), din("w_o_hg", [depth, 512, D]), din("w_o_mem", [depth, 512, D])]
    w_mem_kv = din("w_mem_kv", [depth, D, 1024])
    w_out = din("w_out", [depth, D, D])
    g_ffn = [din("g_ffn1", [128, depth, 8]), din("g_ffn2", [128, depth, 8])]
    g_mix = din("g_mix", [128, depth, 8])
    g_mem = din("g_mem", [128, depth, 8])
    g_q = din("g_q", [128, depth, 3])
    g_kv = din("g_kv", [128, depth, 2])
    g_hg = din("g_hg", [128, depth])
    hlb_in = din("hlb", [128, dtot, 4])
    lsel_in = din("lsel", [128, depth, dtot, 4])
    g_final = din("g_final", [128, 8])
    c_consts = din("c_consts", [128, 8])
    c_tri = din("c_tri", [128, 128])
    c_blk = din("c_blk", [128, 128])
    c_rst = din("c_rst", [128, TT])
    c_ident = din("c_ident", [128, 128])
    yT = nc.dram_tensor("yT", [D, T], F32, kind="ExternalOutput").ap()
    xs = nc.dram_tensor("xs", [D, T], F32, kind="Internal").ap()
    flag_in = din("flag", [128, 1])
    KTs2 = [nc.dram_tensor("KTs%d" % i, [8 * 96, T], BF16, kind="Internal").ap() for i in range(2)]
    Vs2 = [nc.dram_tensor("Vs%d" % i, [8 * 128, PB * 65], BF16, kind="Internal").ap() for i in range(2)]
    Ss = nc.dram_tensor("Ss", [4 * 128, 128], F32, kind="Internal").ap()
    KTg = [nc.dram_tensor("KTg%d" % i, [2 * 2 * 96, T], BF16, kind="Internal").ap() for i in range(4)]
    Vg = [nc.dram_tensor("Vg%d" % i, [2 * 2 * 128, PB * 65], BF16, kind="Internal").ap() for i in range(4)]
    Sg = nc.dram_tensor("Sg", [2 * 4 * 128, 128], F32, kind="Internal").ap()
    RG = [[0, 1], [2, 3], [4, 5], [6, 7]]

    xs_dep = [Dep() for _ in range(NT)]
    y_dep = [Dep() for _ in range(NT)]
    kts_dep2 = [[Dep() for _ in range(NT)] for _ in range(2)]
    vs_dep2 = [[Dep() for _ in range(NT)] for _ in range(2)]
    ss_d, sg_d = Dep(), Dep()
    kg_d = [Dep() for _ in range(4)]
    vg_d = [Dep() for _ in range(4)]

    def const(name, src, shape, dt, eng="sp"):
        t = k.sb(shape, dt, name)
        d = Dep()
        op(eng, lambda e: e.dma_start(out=t[:], in_=src), writes=[d], dma=True)
        return t, d

    mean1024, mean_dep = k.sb([128, 128], BF16, "mean1024"), Dep()
    mean384 = k.sb([128, 128], BF16, "mean384")
    mean256 = k.sb([128, 128], BF16, "mean256")
    mean128 = k.sb([128, 128], BF16, "mean128")
    ones_bf = k.sb([128, 128], BF16, "ones_bf")
    ones_f = k.sb([128, 64], F32, "ones_f")
    for t_, v_ in ((mean1024, 1.0 / 1024), (mean384, 1.0 / 384), (mean256, 1.0 / 256), (mean128, 1.0 / 128),
                   (ones_bf, 1.0), (ones_f, 1.0)):
        op("dve", lambda e: e.memset(t_[:], v_), writes=[mean_dep])
    gsb0, gd0 = const("g_ffn0", g_ffn[0], [128, depth, 8], F32)
    gsb1, gd1 = const("g_ffn1s", g_ffn[1], [128, depth, 8], F32)
    gsb = [gsb0, gsb1]
    gdep = [gd0, gd1]
    gmix, gmix_d = const("g_mixs", g_mix, [128, depth, 8], F32)
    gmem, gmem_d = const("g_mems", g_mem, [128, depth, 8], F32)
    gq, gq_d = const("g_qs", g_q, [128, depth, 3], F32)
    gkv, gkv_d = const("g_kvs", g_kv, [128, depth, 2], F32)
    ghg, ghg_d = const("g_hgs", g_hg, [128, depth], F32)
    gfin, gfin_dep = const("g_fin", g_final, [128, 8], F32)
    cst, cst_d = const("cst", c_consts, [128, 8], F32)
    tri, tri_d = const("tri", c_tri, [128, 128], BF16, eng="pool")
    blk, blk_d = const("blk", c_blk, [128, 128], BF16, eng="pool")
    rst, rst_d = const("rst", c_rst, [128, TT], F32)
    ident, ident_d = const("ident", c_ident, [128, 128], BF16, eng="pool")
    hlb, hlb_d = const("hlbs", hlb_in, [128, dtot, 4], F32)
    lsel, lsel_d = const("lsels", lsel_in, [128, depth, dtot, 4], F32)
    flag, flag_d = const("flags", flag_in, [128, 1], F32)

    lb = k.sb([128, dtot, 4], F32, "lb")
    lbtmp = k.sb([128, 8], F32, "lbtmp")
    lb_d = Dep()
    op("act", lambda e: e.activation(out=hlb[:], in_=hlb[:], func=AF.Exp), reads=[hlb_d], writes=[hlb_d])
    op("dve", lambda e: e.tensor_copy(out=lbtmp[:, 0:4], in_=hlb[:, 0, :]), reads=[hlb_d], writes=[lb_d])
    for l in range(1, dtot):
        op("dve", lambda e: e.tensor_tensor(out=lbtmp[:, 0:4], in0=lbtmp[:, 0:4], in1=hlb[:, l, :], op=ALU.add),
           reads=[hlb_d, lb_d], writes=[lb_d])
    op("dve", lambda e: e.reciprocal(out=lbtmp[:, 4:8], in_=lbtmp[:, 0:4]), reads=[lb_d], writes=[lb_d])
    op("dve", lambda e: e.memset(lb[:, 0, :], 0.0), reads=[lb_d], writes=[lb_d])
    for l in range(1, dtot):
        op("dve", lambda e: e.tensor_tensor(out=lbtmp[:, 0:4], in0=hlb[:, l, :], in1=lbtmp[:, 4:8], op=ALU.mult),
           reads=[hlb_d, lb_d], writes=[lb_d])
        op("dve", lambda e: e.tensor_tensor(out=lb[:, l, :], in0=lb[:, l - 1, :], in1=lbtmp[:, 0:4], op=ALU.add),
           reads=[lb_d], writes=[lb_d])
    lbs = k.sb([128, depth, 4], F32, "lbs")
    oml = k.sb([128, depth, 4], F32, "oml")
    for s_ in range(depth):
        op("dve", lambda e: e.tensor_tensor(out=lsel[:, s_, :, :], in0=lsel[:, s_, :, :], in1=lb[:], op=ALU.mult),
           reads=[lb_d, lsel_d], writes=[lsel_d])
        op("dve", lambda e: e.tensor_copy(out=lbs[:, s_, :], in_=lsel[:, s_, 0, :]), reads=[lsel_d], writes=[lb_d])
        for l in range(1, dtot):
            op("dve", lambda e: e.tensor_tensor(out=lbs[:, s_, :], in0=lbs[:, s_, :], in1=lsel[:, s_, l, :], op=ALU.add),
               reads=[lsel_d, lb_d], writes=[lb_d])
    op("dve", lambda e: e.tensor_scalar(out=oml[:], in0=lbs[:], scalar1=-1.0, scalar2=1.0, op0=ALU.mult, op1=ALU.add),
       reads=[lb_d], writes=[lb_d])

    xtile = k.sb([128, 8, TT], F32, "xt")
    xd = Dep()
    xn = k.sb([128, 8, TT], BF16, "xn")
    xn_dep = Dep()
    rstd = k.sb([128, TT], F32, "rstd")
    rstd_dep = Dep()
    hT = k.sb([128, NFF, TT], BF16, "hT")
    h_dep = [Dep() for _ in range(NFF)]
    hc1 = [(hT[:, c, :], h_dep[c]) for c in range(NFF)]
    f32r = Rot([k.sb([128, TT], F32, "f32r%d" % i) for i in range(8)])
    sa = f32r
    psr = Rot([k.ps([128, TT], F32, "psb%d" % i) for i in range(4)])
    psO = Rot([k.ps([128, TT], F32, "psO%d" % i) for i in range(2)])
    psM = k.ps([128, TT], F32, "psM")
    psM_d = Dep()
    psT = k.ps([128, 2 * TT], BF16, "psT")
    psT_d = Dep()
    WB = 5632
    wbuf = Rot([k.sb([128, WB], BF16, "wb%d" % i) for i in range(3)])

    def load_w(src, kc, n, eng="pool"):
        wt, wd = wbuf.get()
        v = wt[:, 0:kc * n].rearrange("p (a b) -> p a b", a=kc)
        op(eng, lambda e: e.dma_start(out=v, in_=src), writes=[wd], dma=True)
        return v, wd

    def mmacc(pt, pd, pairs, reads, m=None):
        n = len(pairs)
        for i, (lt, rh) in enumerate(pairs):
            op("pe", lambda e: e.matmul(pt, lt, rh, start=(i == 0), stop=(i == n - 1)),
               reads=reads, writes=[pd], inc=(i == n - 1))

    def rms_stats(nch, meanmat, n, sqc=None):
        if sqc is None:
            sqc = hc1
        pt, pd = psr.get()
        mmacc(pt[:, 0:n], pd, [(meanmat[:], sqc[c][0][:, 0:n]) for c in range(nch)], [sqc[c][1] for c in range(nch)] + [mean_dep])
        op("act", lambda e: e.activation(out=rstd[:, 0:n], in_=pt[:, 0:n], func=AF.Sqrt, bias=EPS),
           reads=[pd], writes=[rstd_dep])
        op("dve", lambda e: e.reciprocal(out=rstd[:, 0:n], in_=rstd[:, 0:n]), reads=[rstd_dep], writes=[rstd_dep])

    def rmsnorm(xl, g_ap_fn, gd, outl, n=TT, meanmat=None, sqc=None):
        if sqc is None:
            sqc = hc1
        nch = len(xl)
        for c, (xa, xdl) in enumerate(xl):
            op("act", lambda e: e.activation(out=sqc[c][0][:, 0:n], in_=xa, func=AF.Square), reads=xdl, writes=[sqc[c][1]])
        rms_stats(nch, meanmat if meanmat is not None else mean1024, n, sqc)
        for c, (xa, xdl) in enumerate(xl):
            oa, odl = outl[c]
            op("dve", lambda e: e.scalar_tensor_tensor(out=oa, in0=xa, scalar=g_ap_fn(c), in1=rstd[:, 0:n],
                                                       op0=ALU.mult, op1=ALU.mult),
               reads=xdl + [rstd_dep, gd], writes=odl)

    def xchunks(x, xdl):
        return [(x[:, c, :], xdl) for c in range(8)]

    def xnchunks():
        return [(xn[:, c, :], [xn_dep]) for c in range(8)]

    def ffn(l, which, tiles):
        for tl in tiles:
            rmsnorm(xchunks(tl["x"], tl["xdl"]), lambda c: gsb[which][:, l, c:c + 1], gdep[which], tl["xn"], sqc=tl["hc"])
        win = w_ffn_in[which][l].rearrange("(kc p) n -> p kc n", p=128)
        wout = w_ffn_out[which][l].rearrange("(kc p) n -> p kc n", p=128)
        groups = [(g * 512, 4) for g in range(5)] + [(2560, 2)]
        for c0, nchk in groups:
            va, wad = load_w(win[:, :, c0:c0 + nchk * 128], 8, nchk * 128)
            vb, wbd = load_w(win[:, :, DFF + c0:DFF + c0 + nchk * 128], 8, nchk * 128)
            for j in range(nchk):
                ff = c0 // 128 + j
                for tl in tiles:
                    xnl = tl["xn"]
                    xnd = [d_ for (_, dl_) in xnl for d_ in dl_]
                    pa, pad = psr.get()
                    pb, pbd = psr.get()
                    mmacc(pa[:], pad, [(va[:, c, j * 128:(j + 1) * 128], xnl[c][0]) for c in range(8)], [wad] + xnd)
                    mmacc(pb[:], pbd, [(vb[:, c, j * 128:(j + 1) * 128], xnl[c][0]) for c in range(8)], [wbd] + xnd)
                    st, sd = f32r.get()
                    op("act", lambda e: e.activation(out=st[:], in_=pa[:], func=AF.Silu), reads=[pad], writes=[sd])
                    ha, hd = tl["hc"][ff]
                    op("dve", lambda e: e.tensor_tensor(out=ha, in0=st[:], in1=pb[:], op=ALU.mult),
                       reads=[sd, pbd], writes=[hd])
        for m4 in range(2):
            v0, wd0 = load_w(wout[:, 0:11, m4 * 512:(m4 + 1) * 512], 11, 512)
            v1, wd1 = load_w(wout[:, 11:22, m4 * 512:(m4 + 1) * 512], 11, 512)
            for j in range(4):
                m = m4 * 4 + j
                for tl in tiles:
                    hc = tl["hc"]
                    po, pod = psr.get()
                    pairs = [((v0 if c < 11 else v1)[:, c % 11, j * 128:(j + 1) * 128], hc[c][0]) for c in range(NFF)]
                    mmacc(po[:], pod, pairs, [wd0, wd1] + [hc[c][1] for c in range(NFF)])
                    x = tl["x"]
                    op("dve", lambda e: e.scalar_tensor_tensor(out=x[:, m, :], in0=po[:], scalar=0.5, in1=x[:, m, :],
                                                               op0=ALU.mult, op1=ALU.add),
                       reads=[pod] + tl["xdl"], writes=tl["xdl"])

    cqn = k.sb([128, 3, TT], BF16, "cqn")
    ckvn = k.sb([128, 2, TT], BF16, "ckvn")
    cn_d = Dep()
    Qt = k.sb([128, 8, TT], BF16, "Qt")
    Kt = k.sb([128, 8, TT], BF16, "Kt")
    Vt = k.sb([128, 4, 8, 65], BF16, "Vt")
    Qt_d, Kt_d, Vt_d = Dep(), Dep(), Dep()
    op("pool", lambda e: e.memset(Vt[:], 1.0), writes=[Vt_d])
    rope = k.sb([96, 2, TT], F32, "rope")
    ropei = k.sb([96, TT], I32, "ropei")
    rope_d = Dep()
    wkr = k.sb([128, 8, 2, 96], BF16, "wkr")
    wkr_d = Dep()
    op("pool", lambda e: e.memset(wkr[:], 0.0), writes=[wkr_d])
    wuqs = k.sb([128, 3, 2, 768], BF16, "wuqs")
    wuq_d = Dep()
    op("pool", lambda e: e.memset(wuqs[:], 0.0), writes=[wuq_d])
    wukv = k.sb([128, 2, 2, 512], BF16, "wukv")
    wukv_d = Dep()
    khA = k.sb([128, 2, max(2 * T, 4096)], BF16, "khA")
    kh = Rot([khA[:, i, :] for i in range(2)])
    vh = Rot([k.sb([128, NB, 65], BF16, "vh%d" % i) for i in range(2)])
    Pt = Rot([k.sb([128, TT], BF16, "Pt%d" % i) for i in range(3)])
    ofull, ofull_d = hT[:, 8:12, :], h_dep[8:12]
    hgo, hgo_d = hT[:, 12:16, :], h_dep[12:16]
    omem, omem_d = hT[:, 16:20, :], h_dep[16:20]
    memn, memn_d = hT[:, 8:12, :].rearrange("p a (b c) -> p (a b) c", c=256), h_dep[8:12]
    mx = k.sb([128, 8, TT], BF16, "mx")
    mqT = mx[:, 0:4, :]
    mq_d = Dep()
    memK = k.sb([128, 4, 256], BF16, "memK")
    memV = k.sb([128, 2, 512], BF16, "memV")
    memKV_d = Dep()
    mergedb = k.sb([128, 8, TT], BF16, "mergedb")
    mgb_d = Dep()
    hgb = k.sb([128, TT], BF16, "hgb")
    hgb_d = Dep()
    vtok = mx[:, 4:8, :]
    vtok_d = Dep()
    qtl = k.sb([128, TT], BF16, "qtl")
    ktl = k.sb([128, TT], BF16, "ktl")
    khat = k.sb([128, TT], BF16, "khat")
    khtok = k.sb([128, 4, 128], BF16, "khtok")
    ATb = k.sb([128, 4, 128], BF16, "ATb")
    qtl_d, ktl_d, khat_d, khtok_d, ATb_d = Dep(), Dep(), Dep(), Dep(), Dep()
    S32 = k.sb([128, 4, 128], F32, "S32")
    Sbf = k.sb([128, 4, 128], BF16, "Sbf")
    S_d = [Dep() for _ in range(4)]
    Sb_d = [Dep() for _ in range(4)]

    xsv = xs.rearrange("(c p) t -> p c t", p=128)
    xinv = xT_in.rearrange("(c p) t -> p c t", p=128)
    yv = yT.rearrange("(c p) t -> p c t", p=128)
    memv = memT_in.rearrange("(c p) t -> p c t", p=128)

    def layer_setup(l):
        wi = w_in[l].rearrange("(kc p) n -> p kc n", p=128)
        op("pool", lambda e: e.dma_start(out=wkr[:, :, 0, 64:96], in_=wi[:, :, O_KR:O_KR + 32]), writes=[wkr_d], dma=True)
        op("pool", lambda e: e.dma_start(out=wkr[:, :, 1, 64:80], in_=wi[:, :, O_KR + 16:O_KR + 32]), writes=[wkr_d], dma=True)
        op("pool", lambda e: e.dma_start(out=wkr[:, :, 1, 80:96], in_=wi[:, :, O_KR:O_KR + 16]), writes=[wkr_d], dma=True)
        wq = w_uq[l].rearrange("(kc p) n -> p kc n", p=128)
        op("pool", lambda e: e.dma_start(out=wuqs[:, :, 0, :], in_=wq), writes=[wuq_d], dma=True)
        wq4 = w_uq[l].rearrange("(kc p) (h c) -> p kc h c", p=128, c=96)
        dst4 = wuqs[:, :, 1, :].rearrange("p k (h c) -> p k h c", c=96)
        for kc in range(3):
            op("pool", lambda e: e.dma_start(out=dst4[:, kc, :, 64:80], in_=wq4[:, kc, :, 80:96]), writes=[wuq_d], dma=True)
            op("pool", lambda e: e.dma_start(out=dst4[:, kc, :, 80:96], in_=wq4[:, kc, :, 64:80]), writes=[wuq_d], dma=True)
        op("pool", lambda e: e.dma_start(out=wukv[:, :, 0, :], in_=w_uk[l].rearrange("(kc p) n -> p kc n", p=128)),
           writes=[wukv_d], dma=True)
        op("pool", lambda e: e.dma_start(out=wukv[:, :, 1, :], in_=w_uv[l].rearrange("(kc p) n -> p kc n", p=128)),
           writes=[wukv_d], dma=True)
        ml = []
        for i in range(4):
            ft, fd = f32r.get()
            op("sp", lambda e: e.dma_start(out=ft[:].rearrange("p (a b) -> p a b", a=2), in_=memv[:, 2 * i:2 * i + 2, :]),
               writes=[fd], dma=True)
            ml += [(ft[:, 0:256], [fd]), (ft[:, 256:512], [fd])]
        rmsnorm(ml, lambda c: gmem[:, l, c:c + 1], gmem_d, [(memn[:, c, :], memn_d) for c in range(8)], n=256)
        wkv = w_mem_kv[l].rearrange("(kc p) n -> p kc n", p=128)
        vK, dK = load_w(wkv[:, :, 0:512], 8, 512)
        for h in range(4):
            pt, pd = psr.get()
            mmacc(pt[:, 0:256], pd, [(vK[:, c, h * 128:(h + 1) * 128], memn[:, c, :]) for c in range(8)], [dK] + memn_d)
            op("act", lambda e: e.activation(out=memK[:, h, :], in_=pt[:, 0:256], func=AF.Copy), reads=[pd], writes=[memKV_d])
        vV, dV = load_w(wkv[:, :, 512:1024], 8, 512)
        for mb in range(2):
            pt, pd = psr.get()
            mmacc(pt[:], pd, [(memn[:, c, mb * 128:(mb + 1) * 128], vV[:, c, :]) for c in range(8)], [dV] + memn_d)
            op("act", lambda e: e.activation(out=memV[:, mb, :], in_=pt[:], func=AF.Copy), reads=[pd], writes=[memKV_d])
        op("sp", lambda e: e.dma_start(out=S32[:], in_=Sg[0:512, :].rearrange("(h p) v -> p h v", p=128)),
           reads=[sg_d], writes=S_d, dma=True)
        op("dve", lambda e: e.tensor_scalar(out=S32[:], in0=S32[:], scalar1=flag[:, 0:1], scalar2=None, op0=ALU.mult),
           reads=S_d + [flag_d], writes=S_d)
        op("act", lambda e: e.activation(out=Sbf[:], in_=S32[:], func=AF.Copy), reads=S_d, writes=Sb_d)

    def rope_tables(t):
        R = slice(64, 96)
        scr = [f32r.get() for _ in range(4)]
        ang, tmp, nf, r = [b_[0][R, :] for b_ in scr]
        rd = [rope_d] + [b_[1] for b_ in scr]
        op("sp", lambda e: e.dma_start(out=ropei[R, :], in_=pos_in[t * TT:(t + 1) * TT].partition_broadcast(32)),
           writes=rd, dma=True)
        op("dve", lambda e: e.tensor_copy(out=tmp, in_=ropei[R, :]), reads=rd, writes=rd)
        op("dve", lambda e: e.tensor_scalar(out=ang, in0=tmp, scalar1=cst[R, 0:1], scalar2=None, op0=ALU.mult),
           reads=rd + [cst_d], writes=rd)
        for which in range(2):
            if which == 0:
                op("dve", lambda e: e.tensor_scalar(out=tmp, in0=ang, scalar1=float(np.pi / 2), scalar2=None, op0=ALU.add),
                   reads=rd, writes=rd)
                src = tmp
            else:
                src = ang
            op("dve", lambda e: e.tensor_scalar(out=ropei[R, :], in0=src, scalar1=float(1.0 / (2 * np.pi)), scalar2=None,
                                                op0=ALU.mult), reads=rd, writes=rd)
            op("dve", lambda e: e.tensor_copy(out=nf, in_=ropei[R, :]), reads=rd, writes=rd)
            op("dve", lambda e: e.scalar_tensor_tensor(out=r, in0=nf, scalar=-TWO_PI_HI, in1=src, op0=ALU.mult, op1=ALU.add),
               reads=rd, writes=rd)
            op("dve", lambda e: e.scalar_tensor_tensor(out=r, in0=nf, scalar=-TWO_PI_LO, in1=r, op0=ALU.mult, op1=ALU.add),
               reads=rd, writes=rd)
            op("dve", lambda e: e.tensor_scalar(out=r, in0=r, scalar1=-PI_SAFE, scalar2=PI_SAFE, op0=ALU.max, op1=ALU.min),
               reads=rd, writes=rd)
            op("act", lambda e: e.activation(out=rope[R, which, :], in_=r, func=AF.Sin), reads=rd, writes=rd)
        op("dve", lambda e: e.tensor_scalar(out=rope[R, 1, :], in0=rope[R, 1, :], scalar1=cst[R, 1:2], scalar2=None,
                                            op0=ALU.mult), reads=rd + [cst_d], writes=rd)

    def apply_rope(pA, pAd, pB, pBd, dst, dst_d):
        R = slice(64, 96)
        sc1, sc2 = f32r.get(), f32r.get()
        t1, t2 = sc1[0][R, :], sc2[0][R, :]
        op("dve", lambda e: e.tensor_tensor(out=t1, in0=pA[R, :], in1=rope[R, 0, :], op=ALU.mult),
           reads=[pAd, rope_d], writes=[sc1[1]])
        op("dve", lambda e: e.tensor_tensor(out=t2, in0=pB[R, :], in1=rope[R, 1, :], op=ALU.mult),
           reads=[pBd, rope_d], writes=[sc2[1]])
        op("dve", lambda e: e.tensor_tensor(out=dst, in0=t1, in1=t2, op=ALU.add), reads=[sc1[1], sc2[1]], writes=[dst_d])

    def mla_proj(l, t, wi):
        v, wd = load_w(wi[:, :, 0:640], 8, 640)
        cl = []
        for j in range(5):
            pt, pd = psr.get()
            mmacc(pt[:], pd, [(v[:, c, j * 128:(j + 1) * 128], xn[:, c, :]) for c in range(8)], [wd, xn_dep])
            ft, fd = f32r.get()
            op("act", lambda e: e.activation(out=ft[:], in_=pt[:], func=AF.Copy), reads=[pd], writes=[fd])
            cl.append((ft[:], [fd]))
        rmsnorm(cl[0:3], lambda c: gq[:, l, c:c + 1], gq_d, [(cqn[:, c, :], [cn_d]) for c in range(3)], meanmat=mean384)
        rmsnorm(cl[3:5], lambda c: gkv[:, l, c:c + 1], gkv_d, [(ckvn[:, c, :], [cn_d]) for c in range(2)], meanmat=mean256)
        rope_tables(t)
        pA, pAd = psr.get()
        pB, pBd = psr.get()
        mmacc(pA[0:96, :], pAd, [(wkr[:, c, 0, :], xn[:, c, :]) for c in range(8)], [wkr_d, xn_dep])
        mmacc(pB[0:96, :], pBd, [(wkr[:, c, 1, :], xn[:, c, :]) for c in range(8)], [wkr_d, xn_dep])
        apply_rope(pA, pAd, pB, pBd, Kt[64:96, 0, :], Kt_d)
        for h in range(1, 8):
            op("dve", lambda e: e.tensor_copy(out=Kt[64:96, h, :], in_=Kt[64:96, 0, :]), reads=[Kt_d], writes=[Kt_d])
        for h in range(8):
            pA, pAd = psr.get()
            pB, pBd = psr.get()
            mmacc(pA[0:96, :], pAd, [(wuqs[:, c, 0, h * 96:(h + 1) * 96], cqn[:, c, :]) for c in range(3)], [wuq_d, cn_d])
            mmacc(pB[0:96, :], pBd, [(wuqs[:, c, 1, h * 96:(h + 1) * 96], cqn[:, c, :]) for c in range(3)], [wuq_d, cn_d])
            op("act", lambda e: e.activation(out=Qt[0:64, h, :], in_=pA[0:64, :], func=AF.Copy), reads=[pAd], writes=[Qt_d])
            apply_rope(pA, pAd, pB, pBd, Qt[64:96, h, :], Qt_d)
        for hp in range(4):
            pt, pd = psr.get()
            mmacc(pt[:], pd, [(wukv[:, c, 0, hp * 128:(hp + 1) * 128], ckvn[:, c, :]) for c in range(2)], [wukv_d, cn_d])
            op("act", lambda e: e.activation(out=Kt[0:64, 2 * hp, :], in_=pt[0:64, :], func=AF.Copy), reads=[pd], writes=[Kt_d])
            op("act", lambda e: e.activation(out=Kt[0:64, 2 * hp + 1, :], in_=pt[64:128, :], func=AF.Copy), reads=[pd], writes=[Kt_d])
        for tb in range(4):
            pt, pd = psr.get()
            mmacc(pt[:], pd, [(ckvn[:, c, tb * 128:(tb + 1) * 128], wukv[:, c, 1, :]) for c in range(2)], [wukv_d, cn_d])
            op("act", lambda e: e.activation(out=Vt[:, tb, :, 0:64], in_=pt[:].rearrange("p (h c) -> p h c", c=64),
                                             func=AF.Copy), reads=[pd], writes=[Vt_d])
        par = l % 2
        op("sp", lambda e: e.dma_start(out=KTs2[par].rearrange("(h p) t -> p h t", p=96)[:, :, t * TT:(t + 1) * TT], in_=Kt[0:96, :, :]),
           reads=[Kt_d], writes=[kts_dep2[par][t]], dma=True)
        vs4 = Vs2[par].rearrange("(h p) (b c) -> h p b c", p=128, c=65)
        for h in range(8):
            op("sp", lambda e: e.dma_start(out=vs4[h, :, t * 4:(t + 1) * 4, :], in_=Vt[:, :, h, :]),
               reads=[Vt_d], writes=[vs_dep2[par][t]], dma=True)

    SCALE = float(96 ** -0.5)

    def mla_attn(l, t):
        par = l % 2
        nkb = PB + 4 * (t + 1)
        kts4 = KTs2[par].rearrange("(h p) t -> h p t", p=96)
        vs4 = Vs2[par].rearrange("(h p) (b c) -> h p b c", p=128, c=65)
        ktg4 = [g_.rearrange("(r h p) t -> r h p t", r=2, p=96) for g_ in KTg]
        vg5 = [g_.rearrange("(r h p) (b c) -> r h p b c", r=2, p=128, c=65) for g_ in Vg]
        pending = []

        def make_epi(h, po, pod):
            def epi():
                rrow, rrow_d = f32r.get()
                o32, o32_d = f32r.get()
                op("dve", lambda e: e.reciprocal(out=rrow[64:65, :], in_=po[64:65, :]), reads=[pod], writes=[rrow_d])
                op("pe", lambda e: e.matmul(psM[0:64, :], ones_f[64:65, :], rrow[64:65, :], start=True, stop=True),
                   reads=[rrow_d, mean_dep], writes=[psM_d])
                op("act", lambda e: e.activation(out=o32[0:64, :], in_=po[0:64, :], func=AF.Copy), reads=[pod], writes=[o32_d])
                r0 = (h % 2) * 64
                op("dve", lambda e: e.tensor_tensor(out=ofull[r0:r0 + 64, h // 2, :], in0=o32[0:64, :], in1=psM[0:64, :],
                                                    op=ALU.mult), reads=[o32_d, psM_d], writes=[ofull_d[h // 2]])
            return epi

        for h in range(8):
            kt, kd = kh.get()
            vt, vd = vh.get()
            op("sp", lambda e: e.dma_start(out=kt[0:96, 0:T], in_=ktg4[h // 2][0, h % 2, :, :]), reads=[kg_d[h // 2]], writes=[kd], dma=True)
            op("sp", lambda e: e.dma_start(out=kt[0:96, T:T + (t + 1) * TT], in_=kts4[h, :, 0:(t + 1) * TT]),
               reads=kts_dep2[par][0:t + 1], writes=[kd], dma=True)
            op("sp", lambda e: e.dma_start(out=vt[:, 0:PB, :], in_=vg5[h // 2][0, h % 2, :, :, :]), reads=[vg_d[h // 2]], writes=[vd], dma=True)
            op("sp", lambda e: e.dma_start(out=vt[:, PB:nkb, :], in_=vs4[h, :, 0:4 * (t + 1), :]),
               reads=vs_dep2[par][0:t + 1], writes=[vd], dma=True)
            op("dve", lambda e: e.tensor_scalar(out=vt[:, 0:PB, :], in0=vt[:, 0:PB, :], scalar1=flag[:, 0:1], scalar2=None,
                                                op0=ALU.mult), reads=[vd, flag_d], writes=[vd])
            po, pod = psO.get()

            def q0_of(kb):
                return max(0, kb - PB - 4 * t) * 128

            def emit_S(kb):
                ps_, psd = psr.get()
                q0 = q0_of(kb)
                op("pe", lambda e: e.matmul(ps_[:, q0:TT], kt[0:96, kb * 128:(kb + 1) * 128], Qt[0:96, h, q0:TT],
                                            start=True, stop=True), reads=[kd, Qt_d], writes=[psd])
                return ps_, psd

            nxt = emit_S(0)
            for kb in range(nkb):
                if kb == 3 and pending:
                    pending.pop(0)()
                ps_, psd = nxt
                if kb + 1 < nkb:
                    nxt = emit_S(kb + 1)
                q0 = q0_of(kb)
                p_, p_d = Pt.get()
                op("act", lambda e: e.activation(out=p_[:, q0:TT], in_=ps_[:, q0:TT], func=AF.Exp, scale=SCALE),
                   reads=[psd], writes=[p_d])
                if kb >= PB + 4 * t:
                    op("dve", lambda e: e.tensor_tensor(out=p_[:, q0:q0 + 128], in0=p_[:, q0:q0 + 128], in1=tri[:],
                                                         op=ALU.mult), reads=[p_d, tri_d], writes=[p_d])
                op("pe", lambda e: e.matmul(po[0:65, q0:TT], vt[:, kb, :], p_[:, q0:TT], start=(kb == 0),
                                            stop=(kb == nkb - 1)), reads=[vd, p_d], writes=[pod], inc=True)
            pending.append(make_epi(h, po, pod))
        while pending:
            pending.pop(0)()

    MSCALE = float(128 ** -0.5)

    def mem_attn(l, wi):
        v, wd = load_w(wi[:, :, O_MQ:O_MQ + 512], 8, 512)
        for h in range(4):
            pt, pd = psr.get()
            mmacc(pt[:], pd, [(v[:, c, h * 128:(h + 1) * 128], xn[:, c, :]) for c in range(8)], [wd, xn_dep])
            op("act", lambda e: e.activation(out=mqT[:, h, :], in_=pt[:], func=AF.Copy), reads=[pd], writes=[mq_d])
        for h in range(4):
            po, pod = psO.get()
            pden, pdend = psO.get()
            ps_list = []
            for mb in range(2):
                ps_, psd = psr.get()
                op("pe", lambda e: e.matmul(ps_[:], memK[:, h, mb * 128:(mb + 1) * 128], mqT[:, h, :], start=True, stop=True),
                   reads=[memKV_d, mq_d], writes=[psd])
                p_, p_d = Pt.get()
                op("act", lambda e: e.activation(out=p_[:], in_=ps_[:], func=AF.Exp, scale=MSCALE), reads=[psd], writes=[p_d])
                ps_list.append((p_, p_d))
            for mb in range(2):
                p_, p_d = ps_list[mb]
                op("pe", lambda e: e.matmul(po[:], memV[:, mb, h * 128:(h + 1) * 128], p_[:], start=(mb == 0), stop=(mb == 1)),
                   reads=[memKV_d, p_d], writes=[pod])
            for mb in range(2):
                p_, p_d = ps_list[mb]
                op("pe", lambda e: e.matmul(pden[:], ones_bf[:], p_[:], start=(mb == 0), stop=(mb == 1)),
                   reads=[mean_dep, p_d], writes=[pdend])
            st, sd = f32r.get()
            op("dve", lambda e: e.reciprocal(out=st[:], in_=pden[:]), reads=[pdend], writes=[sd])
            op("dve", lambda e: e.tensor_tensor(out=omem[:, h, :], in0=st[:], in1=po[:], op=ALU.mult),
               reads=[sd, pod], writes=[omem_d[h]])

    def hgrn(l, t, wi):
        v, wd = load_w(wi[:, :, O_HI:O_HI + 512], 8, 512)
        for tb in range(4):
            pt, pd = psr.get()
            mmacc(pt[:], pd, [(xn[:, c, tb * 128:(tb + 1) * 128], v[:, c, :]) for c in range(8)], [wd, xn_dep])
            op("act", lambda e: e.activation(out=vtok[:, tb, :], in_=pt[:], func=AF.Copy), reads=[pd], writes=[vtok_d])
        sets = [dict(qtl=qtl[:], qd=qtl_d, ktl=ktl[:], kd=ktl_d, khat=khat[:], khd=khat_d, hgb=hgb[:], hd=hgb_d),
                dict(qtl=hT[:, 1, :], qd=h_dep[1], ktl=hT[:, 2, :], kd=h_dep[2], khat=hT[:, 3, :], khd=h_dep[3],
                     hgb=hT[:, 4, :], hd=h_dep[4])]

        def chain(h):
            B = sets[h % 2]
            wt, wd = wbuf.get()
            v = wt[:, 0:8 * 384].rearrange("p (a b) -> p a b", a=8)
            for i, off in enumerate((O_HQ, O_HF, O_HG)):
                op("pool", lambda e: e.dma_start(out=v[:, :, i * 128:(i + 1) * 128],
                                                 in_=wi[:, :, off + h * 128:off + (h + 1) * 128]), writes=[wd], dma=True)
            q_, q_d = f32r.get()
            f_, f_d = f32r.get()
            g_, g_d = f32r.get()
            G_, G_d = f32r.get()
            E_, E_d = f32r.get()
            for i, (dst, dd, fn) in enumerate(((q_[:], q_d, AF.Silu), (f_[:], f_d, AF.Sigmoid), (B["hgb"], B["hd"], AF.Silu))):
                pt, pd = psr.get()
                mmacc(pt[:], pd, [(v[:, c, i * 128:(i + 1) * 128], xn[:, c, :]) for c in range(8)], [wd, xn_dep])
                op("act", lambda e: e.activation(out=dst, in_=pt[:], func=fn), reads=[pd], writes=[dd])
            op("dve", lambda e: e.tensor_scalar(out=f_[:], in0=f_[:], scalar1=oml[:, l, h:h + 1], scalar2=lbs[:, l, h:h + 1],
                                                op0=ALU.mult, op1=ALU.add), reads=[f_d, lb_d], writes=[f_d])
            op("act", lambda e: e.activation(out=g_[:], in_=f_[:], func=AF.Ln), reads=[f_d], writes=[g_d])
            op("dve", lambda e: e.tensor_tensor_scan(out=G_[:], data0=rst[:], data1=g_[:], initial=0.0, op0=ALU.mult, op1=ALU.add),
               reads=[g_d, rst_d], writes=[G_d])
            op("act", lambda e: e.activation(out=E_[:], in_=G_[:], func=AF.Exp), reads=[G_d], writes=[E_d])
            op("dve", lambda e: e.tensor_tensor(out=B["qtl"], in0=q_[:], in1=E_[:], op=ALU.mult),
               reads=[q_d, E_d], writes=[B["qd"]])
            op("act", lambda e: e.activation(out=g_[:], in_=G_[:], func=AF.Exp, scale=-1.0), reads=[G_d, g_d], writes=[g_d])
            op("dve", lambda e: e.tensor_scalar(out=f_[:], in0=f_[:], scalar1=-1.0, scalar2=1.0, op0=ALU.mult, op1=ALU.add),
               reads=[f_d], writes=[f_d])
            k32, k32_d = q_, q_d
            op("dve", lambda e: e.tensor_tensor(out=k32[:], in0=f_[:], in1=g_[:], op=ALU.mult), reads=[f_d, g_d], writes=[k32_d])
            op("act", lambda e: e.activation(out=B["ktl"], in_=k32[:], func=AF.Copy), reads=[k32_d], writes=[B["kd"]])
            for ci in range(8):
                cs = slice(ci * 64, (ci + 1) * 64)
                op("dve", lambda e: e.tensor_scalar(out=B["khat"][:, cs], in0=k32[:, cs], scalar1=E_[:, ci * 64 + 63:ci * 64 + 64],
                                                    scalar2=None, op0=ALU.mult), reads=[k32_d, E_d], writes=[B["khd"]])
            return dict(B=B, E_=E_, E_d=E_d)

        def rest(h, st_):
            B, E_, E_d = st_["B"], st_["E_"], st_["E_d"]
            for tb in range(4):
                op("pe", lambda e: e.transpose(psT[:, tb * 128:(tb + 1) * 128], B["khat"][:, tb * 128:(tb + 1) * 128], ident[:]),
                   reads=[B["khd"], ident_d], writes=[psT_d], inc=(tb == 3))
            op("act", lambda e: e.activation(out=khtok[:].rearrange("p a b -> p (a b)"), in_=psT[:, 0:512], func=AF.Copy),
               reads=[psT_d], writes=[khtok_d])
            pat, patd = psr.get()
            for tb in range(4):
                bs = slice(tb * 128, (tb + 1) * 128)
                op("pe", lambda e: e.matmul(pat[:, bs], B["ktl"][:, bs], B["qtl"][:, bs], start=True, stop=True),
                   reads=[B["kd"], B["qd"]], writes=[patd], inc=(tb == 3))
            for tb in range(4):
                bs = slice(tb * 128, (tb + 1) * 128)
                op("dve", lambda e: e.tensor_tensor(out=ATb[:, tb, :], in0=pat[:, bs], in1=blk[:], op=ALU.mult),
                   reads=[patd, blk_d], writes=[ATb_d])
            po, pod = psO.get()
            for tb in range(4):
                bs = slice(tb * 128, (tb + 1) * 128)
                op("pe", lambda e: e.matmul(po[:, bs], vtok[:, tb, h * 128:(h + 1) * 128], ATb[:, tb, :], start=True, stop=False),
                   reads=[vtok_d, ATb_d], writes=[pod], inc=False)
                for half in range(2):
                    ci = tb * 2 + half
                    cs = slice(ci * 64, (ci + 1) * 64)
                    rs = slice(half * 64, half * 64 + 64)
                    last = (half == 1)
                    op("pe", lambda e: e.matmul(po[:, cs], Sbf[:, h, :], B["qtl"][:, cs], start=False, stop=last),
                       reads=[Sb_d[h], B["qd"]], writes=[pod], inc=True)
                    pu, pud = psr.get()
                    op("pe", lambda e: e.matmul(pu[:, 0:128], khtok[rs, tb, :], vtok[rs, tb, h * 128:(h + 1) * 128],
                                                start=True, stop=True), reads=[khtok_d, vtok_d], writes=[pud])
                    op("dve", lambda e: e.scalar_tensor_tensor(out=S32[:, h, :], in0=S32[:, h, :],
                                                               scalar=E_[:, ci * 64 + 63:ci * 64 + 64], in1=pu[:, 0:128],
                                                               op0=ALU.mult, op1=ALU.add),
                       reads=[S_d[h], E_d, pud], writes=[S_d[h]])
                    op("act", lambda e: e.activation(out=Sbf[:, h, :], in_=S32[:, h, :], func=AF.Copy), reads=[S_d[h]], writes=[Sb_d[h]])
            o32, o32_d = f32r.get()
            st, sd = f32r.get()
            op("act", lambda e: e.activation(out=o32[:], in_=po[:], func=AF.Copy), reads=[pod], writes=[o32_d])
            op("act", lambda e: e.activation(out=hT[:, 0, :], in_=po[:], func=AF.Square), reads=[pod], writes=[h_dep[0]])
            rms_stats(1, mean128, TT)
            op("dve", lambda e: e.scalar_tensor_tensor(out=st[:], in0=o32[:], scalar=ghg[:, l:l + 1], in1=rstd[:],
                                                       op0=ALU.mult, op1=ALU.mult), reads=[o32_d, rstd_dep, ghg_d], writes=[sd])
            op("dve", lambda e: e.tensor_tensor(out=hgo[:, h, :], in0=st[:], in1=B["hgb"], op=ALU.mult),
               reads=[sd, B["hd"]], writes=[hgo_d[h]])

        states = {0: chain(0)}
        for h in range(4):
            if h + 1 < 4:
                states[h + 1] = chain(h + 1)
            rest(h, states.pop(h))

    def merge_out(l, x, xdep, wi):
        branches = ((0, ofull, ofull_d), (1, hgo, hgo_d), (2, omem, omem_d))
        for b, ob, obd in branches:
            if not cfg.get("br%d" % b, True):
                continue
            wo_v = w_o[b][l].rearrange("(kc p) n -> p kc n", p=128)
            for half in range(2):
                vo, wod = load_w(wo_v[:, :, half * 512:(half + 1) * 512], 4, 512)
                vg, wgd = load_w(wi[:, :, O_GATE + b * 1024 + half * 512:O_GATE + b * 1024 + (half + 1) * 512], 8, 512)
                for j in range(4):
                    m = half * 4 + j
                    py, pyd = psr.get()
                    pg, pgd = psr.get()
                    mmacc(py[:], pyd, [(vo[:, c, j * 128:(j + 1) * 128], ob[:, c, :]) for c in range(4)], [wod] + obd)
                    mmacc(pg[:], pgd, [(vg[:, c, j * 128:(j + 1) * 128], xn[:, c, :]) for c in range(8)], [wgd, xn_dep])
                    st, sd = f32r.get()
                    op("act", lambda e: e.activation(out=st[:], in_=pg[:], func=AF.Sigmoid), reads=[pgd], writes=[sd])
                    if b == 0:
                        op("dve", lambda e: e.tensor_tensor(out=mergedb[:, m, :], in0=st[:], in1=py[:], op=ALU.mult),
                           reads=[sd, pyd], writes=[mgb_d])
                    else:
                        op("dve", lambda e: e.tensor_tensor(out=st[:], in0=st[:], in1=py[:], op=ALU.mult),
                           reads=[sd, pyd], writes=[sd])
                        op("dve", lambda e: e.tensor_tensor(out=mergedb[:, m, :], in0=mergedb[:, m, :], in1=st[:], op=ALU.add),
                           reads=[sd, mgb_d], writes=[mgb_d])
        wov = w_out[l].rearrange("(kc p) n -> p kc n", p=128)
        for half in range(2):
            v, wd = load_w(wov[:, :, half * 512:(half + 1) * 512], 8, 512)
            for j in range(4):
                m = half * 4 + j
                pt, pd = psr.get()
                mmacc(pt[:], pd, [(v[:, c, j * 128:(j + 1) * 128], mergedb[:, c, :]) for c in range(8)], [wd, mgb_d])
                op("dve", lambda e: e.tensor_tensor(out=x[:, m, :], in0=x[:, m, :], in1=pt[:], op=ALU.add),
                   reads=[pd, xdep], writes=[xdep])

    def mixer(l, t, x, xdep):
        rmsnorm(xchunks(x, [xdep]), lambda c: gmix[:, l, c:c + 1], gmix_d, xnchunks())
        wi = w_in[l].rearrange("(kc p) n -> p kc n", p=128)
        if cfg.get("br0", True):
            mla_proj(l, t, wi)
            mla_attn(l, t)
        if cfg.get("br1", True):
            hgrn(l, t, wi)
        if cfg.get("br2", True):
            mem_attn(l, wi)
        merge_out(l, x, xdep, wi)

    zt, zd = hT[:, 0:4, :].rearrange("p a b -> p (a b)"), h_dep[0:4]
    op("dve", lambda e: e.memset(zt, 0.0), writes=zd)
    zf, zfd = f32r.get()
    op("dve", lambda e: e.memset(zf[:], 0.0), writes=[zfd])
    for pc in range(4):
        for i in range(3):
            op("sp", lambda e: e.dma_start(out=KTg[pc][i * 128:(i + 1) * 128, :], in_=zt[:, 0:T]), reads=zd, writes=[kg_d[pc]], dma=True)
        for i in range(4):
            op("sp", lambda e: e.dma_start(out=Vg[pc][i * 128:(i + 1) * 128, :], in_=zt[:, 0:PB * 65]), reads=zd, writes=[vg_d[pc]], dma=True)
    for i in range(8):
        op("sp", lambda e: e.dma_start(out=Sg[i * 128:(i + 1) * 128, :], in_=zf[:, 0:128]), reads=[zfd], writes=[sg_d], dma=True)

    def exchange(l):
        par = l % 2
        op("sp", lambda e: e.dma_start(out=Ss.rearrange("(h p) v -> p h v", p=128), in_=S32[:]), reads=S_d, writes=[ss_d], dma=True)
        for pc in range(4):
            op("pool", lambda e: e.collective_compute("AllGather", ALU.bypass, replica_groups=RG,
                                                      ins=[KTs2[par][pc * 192:(pc + 1) * 192, :]], outs=[KTg[pc]]),
               reads=kts_dep2[par], writes=[kg_d[pc]], dma=True, key=("ccK%d" % pc, "c"))
            op("pool", lambda e: e.collective_compute("AllGather", ALU.bypass, replica_groups=RG,
                                                      ins=[Vs2[par][pc * 256:(pc + 1) * 256, :]], outs=[Vg[pc]]),
               reads=vs_dep2[par], writes=[vg_d[pc]], dma=True, key=("ccV%d" % pc, "c"))
        op("pool", lambda e: e.collective_compute("AllGather", ALU.bypass, replica_groups=RG, ins=[Ss],
                                                  outs=[Sg]),
           reads=[ss_d], writes=[sg_d], dma=True, key=("ccS", "c"))

    khdeps = [kh.t[0][1], kh.t[1][1]]
    x_b = khA[:].rearrange("p a b -> p (a b)").bitcast(F32).rearrange("p (c t) -> p c t", c=8)
    xn_b = [(mx[:, c, :], [mq_d if c < 4 else vtok_d]) for c in range(8)]
    hc2 = [(Qt[:, c, :], Qt_d) for c in range(8)] + [(Kt[:, c, :], Kt_d) for c in range(8)] + \
          [(mergedb[:, c, :], mgb_d) for c in range(6)]
    tile_a = dict(x=xtile[:], xdl=[xd], xn=xnchunks(), hc=hc1)
    tile_b = dict(x=x_b, xdl=khdeps, xn=xn_b, hc=hc2)

    def ffn_phase(l_prev, l_next, final_out):
        for p in range(NT // 2):
            tls = [tile_a, tile_b]
            for i, tl in enumerate(tls):
                t = 2 * p + i
                src = xinv if (l_prev is None) else xsv
                op("sp", lambda e: e.dma_start(out=tl["x"], in_=src[:, :, t * TT:(t + 1) * TT]),
                   reads=[xs_dep[t]], writes=tl["xdl"], dma=True)
            if l_prev is not None:
                ffn(l_prev, 1, tls)
            if l_next is not None:
                ffn(l_next, 0, tls)
            for i, tl in enumerate(tls):
                t = 2 * p + i
                if final_out:
                    rmsnorm(xchunks(tl["x"], tl["xdl"]), lambda c: gfin[:, c:c + 1], gfin_dep, xchunks(tl["x"], tl["xdl"]),
                            sqc=tl["hc"])
                    op("sp", lambda e: e.dma_start(out=yv[:, :, t * TT:(t + 1) * TT], in_=tl["x"]),
                       reads=tl["xdl"], writes=[y_dep[t]], dma=True)
                else:
                    op("sp", lambda e: e.dma_start(out=xsv[:, :, t * TT:(t + 1) * TT], in_=tl["x"]),
                       reads=tl["xdl"], writes=[xs_dep[t]], dma=True)

    for l in range(depth):
        ffn_phase(None if l == 0 else l - 1, l, False)
        layer_setup(l)
        for t in range(NT):
            x = xtile
            op("sp", lambda e: e.dma_start(out=x[:], in_=xsv[:, :, t * TT:(t + 1) * TT]),
               reads=[xs_dep[t]], writes=[xd], dma=True)
            mixer(l, t, x, xd)
            op("sp", lambda e: e.dma_start(out=xsv[:, :, t * TT:(t + 1) * TT], in_=x[:]),
               reads=[xd], writes=[xs_dep[t]], dma=True)
        if l < depth - 1:
            exchange(l)
    ffn_phase(depth - 1, None, True)
    k.finish(y_dep)
    return nc, es


def host_inputs(inp, b, half, zc, depth=4, TL=2048):
    f = np.float32
    NS = depth + 1

    def stack(a):
        a = np.asarray(a, f)
        key = (id(a), half)
        z = np.zeros((1,) + a.shape[1:], f)
        return np.concatenate([a, z], 0) if half == 0 else np.concatenate([z, a], 0)

    def fm(g, n):
        return np.ascontiguousarray(stack(g).reshape(NS, n, 128).transpose(2, 0, 1))

    consts = np.zeros((128, 8), f)
    inv_freq = (np.float32(10000.0) ** (-np.arange(0, 32, 2, dtype=np.float32) / np.float32(32))).astype(f)
    consts[64:80, 0] = inv_freq
    consts[80:96, 0] = inv_freq
    consts[64:80, 1] = -1.0
    consts[80:96, 1] = 1.0
    idx = np.arange(128)
    tri = (idx[None, :] >= idx[:, None]).astype(f)
    blk = tri * ((idx[None, :] // 64) == (idx[:, None] // 64))
    rst = np.ones((128, TT), f)
    rst[:, ::64] = 0.0
    onehot = np.zeros((128, NS, depth, 4), f)
    for l in range(depth):
        onehot[:, l + half, l, :] = 1.0
    sl = slice(half * TL, (half + 1) * TL)
    m = dict(
        xT=np.ascontiguousarray(np.asarray(inp["x"][b], f)[sl].T),
        memT=np.ascontiguousarray(np.asarray(inp["mem"][b], f).T),
        pos=np.ascontiguousarray(np.asarray(inp["positions"][b], np.int32)[sl]),
        flag=np.full((128, 1), float(half), f),
        g_ffn1=fm(inp["ffn1_norm"], 8), g_ffn2=fm(inp["ffn2_norm"], 8), g_mix=fm(inp["mix_norm"], 8),
        g_mem=fm(inp["mem_norm"], 8), g_q=fm(inp["q_lat_norm"], 3), g_kv=fm(inp["kv_lat_norm"], 2),
        g_hg=np.ascontiguousarray(stack(inp["hg_out_norm"]).T),
        hlb=np.ascontiguousarray(np.asarray(inp["hg_lower_bounds"], f).reshape(depth, 4, 128).transpose(2, 0, 1)),
        lsel=onehot,
        g_final=np.ascontiguousarray(np.asarray(inp["final_norm"], f).reshape(8, 128).T),
        c_consts=consts, c_tri=tri, c_blk=blk.astype(f), c_rst=rst, c_ident=np.eye(128, dtype=f),
    )
    for nm in ("w_ffn1_in", "w_ffn1_out", "w_ffn2_in", "w_ffn2_out", "w_in", "w_uq", "w_uk", "w_uv", "w_o_mla",
               "w_o_hg", "w_o_mem", "w_mem_kv", "w_out"):
        if (nm, half) not in zc:
            zc[(nm, half)] = stack(inp[nm])
        m[nm] = zc[(nm, half)]
    return m


def kernel(**inp):
    B, S, depth = 4, 4096, 4
    TL = S // 2
    nc, es = build(dict(T=TL, depth=depth + 1, dtot=depth))
    zc = {}
    maps = [host_inputs(inp, core // 2, core % 2, zc) for core in range(8)]
    res = run_bass_kernel_spmd(nc, maps, core_ids=list(range(8)))
    out = np.empty((B, S, D), np.float32)
    for core in range(8):
        b, half = core // 2, core % 2
        out[b, half * TL:(half + 1) * TL, :] = res.results[core]["yT"].T
    return out
```

```python
import contextlib
import numpy as np
import concourse.bass as bass
import concourse.mybir as mybir
from concourse.bass_utils import run_bass_kernel_spmd

F32 = mybir.dt.float32
BF16 = mybir.dt.bfloat16
I32 = mybir.dt.int32
AF = mybir.ActivationFunctionType
ALU = mybir.AluOpType
AX = mybir.AxisListType

D = 1024
DFF = 2816
NFF = DFF // 128
DIN = 6304
TT = 512
EPS = 1e-6
O_CQ, O_CKV, O_KR, O_HQ, O_HF, O_HI, O_HG, O_MQ, O_GATE = 0, 384, 640, 672, 1184, 1696, 2208, 2720, 3232


class Dep:
    __slots__ = ("w", "r")

    def __init__(self):
        self.w = None
        self.r = {}


class Kern:
    def __init__(self, nc, es):
        self.nc = nc
        self.es = es
        self.eng = dict(pe=nc.tensor, act=nc.scalar, dve=nc.vector, pool=nc.gpsimd, sp=nc.sync)
        self.sem = {}
        self.cnt = {}
        self.seen = {e: {} for e in self.eng}
        for e in ("pe", "act", "dve", "pool"):
            self._mk((e, "c"))
        self.NDS = 8
        self.drr = {}
        for e in ("sp", "pool"):
            self.drr[e] = 0
            for j in range(self.NDS):
                self._mk((e, "d", j))
        self.nbuf = 0

    def _mk(self, key):
        self.sem[key] = self.es.enter_context(self.nc.semaphore("s_" + "_".join(str(x) for x in key)))
        self.cnt[key] = 0

    def sb(self, shape, dt, name=None):
        self.nbuf += 1
        t = self.es.enter_context(self.nc.sbuf_tensor("S_" + (name or ("sb%d" % self.nbuf)), list(shape), dt))
        return t

    def ps(self, shape, dt=F32, name=None):
        self.nbuf += 1
        t = self.es.enter_context(self.nc.psum_tensor("P_" + (name or ("ps%d" % self.nbuf)), list(shape), dt))
        return t

    def op(self, e, fn, reads=(), writes=(), inc=True, dma=False, key=None):
        eng = self.eng[e]
        if key is None:
            if dma:
                j = self.drr[e] % self.NDS
                self.drr[e] += 1
                key = (e, "d", j)
                if self.cnt[key] > 0 and self.seen[e].get(key, 0) < self.cnt[key]:
                    eng.wait_ge(self.sem[key], self.cnt[key])
                    self.seen[e][key] = self.cnt[key]
            else:
                key = (e, "c")
        elif key not in self.sem:
            self._mk(key)
        need = {}
        for b in reads:
            if b.w is not None:
                need[b.w[0]] = max(need.get(b.w[0], 0), b.w[1])
        for b in writes:
            if b.w is not None:
                need[b.w[0]] = max(need.get(b.w[0], 0), b.w[1])
            for k, v in b.r.items():
                need[k] = max(need.get(k, 0), v)
        for k, v in need.items():
            if k == ("pe", "c") and e == "pe" and not dma:
                continue
            if self.seen[e].get(k, 0) >= v:
                continue
            eng.wait_ge(self.sem[k], v)
            self.seen[e][k] = v
        ins = fn(eng)
        step = 16 if (dma and key[1] == "d") else 1
        if inc:
            ins.then_inc(self.sem[key], step)
            self.cnt[key] += step
            tok = (key, self.cnt[key])
        else:
            tok = (key, self.cnt[key] + step)
        for b in reads:
            b.r[tok[0]] = max(b.r.get(tok[0], 0), tok[1])
        for b in writes:
            b.w = tok
            b.r = {}
        return ins

    def finish(self, deps):
        eng = self.eng["sp"]
        for b in deps:
            if b.w is not None:
                k, v = b.w
                if self.seen["sp"].get(k, 0) < v:
                    eng.wait_ge(self.sem[k], v)
                    self.seen["sp"][k] = v


class Rot:
    def __init__(self, tiles):
        self.t = [(t, Dep()) for t in tiles]
        self.i = 0

    def get(self):
        r = self.t[self.i % len(self.t)]
        self.i += 1
        return r


TWO_PI_HI = 6.28125
TWO_PI_LO = float(2.0 * np.pi - 6.28125)
PI_SAFE = 3.1415925


def build(cfg):
    T = cfg["T"]
    depth = cfg["depth"]
    dtot = cfg.get("dtot", depth)
    sel = cfg.get("sel", False)
    final = cfg.get("final", True)
    NT = T // TT
    PB = T // 128
    NB = 2 * PB
    nc = bass.Bass("TRN2", target_bir_lowering=False)
    es = contextlib.ExitStack()
    k = Kern(nc, es)
    op = k.op

    def din(name, shape, dt=F32):
        return nc.dram_tensor(name, list(shape), dt, kind="ExternalInput").ap()

    xT_in = din("xT", [D, T])
    memT_in = din("memT", [D, 256])
    pos_in = din("pos", [T], I32)
    w_ffn_in = [din("w_ffn1_in", [depth, D, 2 * DFF]), din("w_ffn2_in", [depth, D, 2 * DFF])]
    w_ffn_out = [din("w_ffn1_out", [depth, DFF, D]), din("w_ffn2_out", [depth, DFF, D])]
    w_in = din("w_in", [depth, D, DIN])
    w_uq = din("w_uq", [depth, 384, 768])
    w_uk = din("w_uk", [depth, 256, 512])
    w_uv = din("w_uv", [depth, 256, 512])
    w_o = [din("w_o_mla", [depth, 512, D]), din("w_o_hg", [depth, 512, D]), din("w_o_mem", [depth, 512, D])]
    w_mem_kv = din("w_mem_kv", [depth, D, 1024])
    w_out = din("w_out", [depth, D, D])
    g_ffn = [din("g_ffn1", [128, depth, 8]), din("g_ffn2", [128, depth, 8])]
    g_mix = din("g_mix", [128, depth, 8])
    g_mem = din("g_mem", [128, depth, 8])
    g_q = din("g_q", [128, depth, 3])
    g_kv = din("g_kv", [128, depth, 2])
    g_hg = din("g_hg", [128, depth])
    hlb_in = din("hlb", [128, dtot, 4])
    lsel_in = din("lsel", [128, depth, dtot, 4])
    g_final = din("g_final", [128, 8])
    c_consts = din("c_consts", [128, 8])
    c_tri = din("c_tri", [128, 128])
    c_blk = din("c_blk", [128, 128])
    c_rst = din("c_rst", [128, TT])
    c_ident = din("c_ident", [128, 128])
    yT = nc.dram_tensor("yT", [D, T], F32, kind="ExternalOutput").ap()
    xs = nc.dram_tensor("xs", [D, T], F32, kind="Internal").ap()
    flag_in = din("flag", [128, 1])
    KTs2 = [nc.dram_tensor("KTs%d" % i, [8 * 96, T], BF16, kind="Internal").ap() for i in range(2)]
    Vs2 = [nc.dram_tensor("Vs%d" % i, [8 * 128, PB * 65], BF16, kind="Internal").ap() for i in range(2)]
    Ss = nc.dram_tensor("Ss", [4 * 128, 128], F32, kind="Internal").ap()
    KTg = [nc.dram_tensor("KTg%d" % i, [2 * 2 * 96, T], BF16, kind="Internal").ap() for i in range(4)]
    Vg = [nc.dram_tensor("Vg%d" % i, [2 * 2 * 128, PB * 65], BF16, kind="Internal").ap() for i in range(4)]
    Sg = nc.dram_tensor("Sg", [2 * 4 * 128, 128], F32, kind="Internal").ap()
    RG = [[0, 1], [2, 3], [4, 5], [6, 7]]
    ropeD = nc.dram_tensor("ropeD", [NT, 32, 2, TT], F32, kind="Internal").ap()
    ropeD_d = Dep()

    xs_dep = [Dep() for _ in range(NT)]
    y_dep = [Dep() for _ in range(NT)]
    kts_dep2 = [[Dep() for _ in range(NT)] for _ in range(2)]
    vs_dep2 = [[Dep() for _ in range(NT)] for _ in range(2)]
    ss_d, sg_d = Dep(), Dep()
    kg_d = [Dep() for _ in range(4)]
    vg_d = [Dep() for _ in range(4)]

    def const(name, src, shape, dt, eng="sp"):
        t = k.sb(shape, dt, name)
        d = Dep()
        op(eng, lambda e: e.dma_start(out=t[:], in_=src), writes=[d], dma=True)
        return t, d

    mean1024, mean_dep = k.sb([128, 128], BF16, "mean1024"), Dep()
    mean384 = k.sb([128, 128], BF16, "mean384")
    mean256 = k.sb([128, 128], BF16, "mean256")
    mean128 = k.sb([128, 128], BF16, "mean128")
    ones_bf = k.sb([128, 128], BF16, "ones_bf")
    ones_f = k.sb([128, 64], F32, "ones_f")
    for t_, v_ in ((mean1024, 1.0 / 1024), (mean384, 1.0 / 384), (mean256, 1.0 / 256), (mean128, 1.0 / 128),
                   (ones_bf, 1.0), (ones_f, 1.0)):
        op("dve", lambda e: e.memset(t_[:], v_), writes=[mean_dep])
    gsb0, gd0 = const("g_ffn0", g_ffn[0], [128, depth, 8], F32)
    gsb1, gd1 = const("g_ffn1s", g_ffn[1], [128, depth, 8], F32)
    gsb = [gsb0, gsb1]
    gdep = [gd0, gd1]
    gmix, gmix_d = const("g_mixs", g_mix, [128, depth, 8], F32)
    gmem, gmem_d = const("g_mems", g_mem, [128, depth, 8], F32)
    gq, gq_d = const("g_qs", g_q, [128, depth, 3], F32)
    gkv, gkv_d = const("g_kvs", g_kv, [128, depth, 2], F32)
    ghg, ghg_d = const("g_hgs", g_hg, [128, depth], F32)
    gfin, gfin_dep = const("g_fin", g_final, [128, 8], F32)
    cst, cst_d = const("cst", c_consts, [128, 8], F32)
    tri, tri_d = const("tri", c_tri, [128, 128], BF16, eng="pool")
    blk, blk_d = const("blk", c_blk, [128, 128], BF16, eng="pool")
    rst, rst_d = const("rst", c_rst, [128, TT], F32)
    ident, ident_d = const("ident", c_ident, [128, 128], BF16, eng="pool")
    hlb, hlb_d = const("hlbs", hlb_in, [128, dtot, 4], F32)
    lsel, lsel_d = const("lsels", lsel_in, [128, depth, dtot, 4], F32)
    flag, flag_d = const("flags", flag_in, [128, 1], F32)

    lb = k.sb([128, dtot, 4], F32, "lb")
    lbtmp = k.sb([128, 8], F32, "lbtmp")
    lb_d = Dep()
    op("act", lambda e: e.activation(out=hlb[:], in_=hlb[:], func=AF.Exp), reads=[hlb_d], writes=[hlb_d])
    op("dve", lambda e: e.tensor_copy(out=lbtmp[:, 0:4], in_=hlb[:, 0, :]), reads=[hlb_d], writes=[lb_d])
    for l in range(1, dtot):
        op("dve", lambda e: e.tensor_tensor(out=lbtmp[:, 0:4], in0=lbtmp[:, 0:4], in1=hlb[:, l, :], op=ALU.add),
           reads=[hlb_d, lb_d], writes=[lb_d])
    op("dve", lambda e: e.reciprocal(out=lbtmp[:, 4:8], in_=lbtmp[:, 0:4]), reads=[lb_d], writes=[lb_d])
    op("dve", lambda e: e.memset(lb[:, 0, :], 0.0), reads=[lb_d], writes=[lb_d])
    for l in range(1, dtot):
        op("dve", lambda e: e.tensor_tensor(out=lbtmp[:, 0:4], in0=hlb[:, l, :], in1=lbtmp[:, 4:8], op=ALU.mult),
           reads=[hlb_d, lb_d], writes=[lb_d])
        op("dve", lambda e: e.tensor_tensor(out=lb[:, l, :], in0=lb[:, l - 1, :], in1=lbtmp[:, 0:4], op=ALU.add),
           reads=[lb_d], writes=[lb_d])
    lbs = k.sb([128, depth, 4], F32, "lbs")
    oml = k.sb([128, depth, 4], F32, "oml")
    for s_ in range(depth):
        op("dve", lambda e: e.tensor_tensor(out=lsel[:, s_, :, :], in0=lsel[:, s_, :, :], in1=lb[:], op=ALU.mult),
           reads=[lb_d, lsel_d], writes=[lsel_d])
        op("dve", lambda e: e.tensor_copy(out=lbs[:, s_, :], in_=lsel[:, s_, 0, :]), reads=[lsel_d], writes=[lb_d])
        for l in range(1, dtot):
            op("dve", lambda e: e.tensor_tensor(out=lbs[:, s_, :], in0=lbs[:, s_, :], in1=lsel[:, s_, l, :], op=ALU.add),
               reads=[lsel_d, lb_d], writes=[lb_d])
    op("dve", lambda e: e.tensor_scalar(out=oml[:], in0=lbs[:], scalar1=-1.0, scalar2=1.0, op0=ALU.mult, op1=ALU.add),
       reads=[lb_d], writes=[lb_d])

    xtile = k.sb([128, 8, TT], F32, "xt")
    xd = Dep()
    xn = k.sb([128, 8, TT], BF16, "xn")
    xn_dep = Dep()
    rstd = k.sb([128, TT], F32, "rstd")
    rstd_dep = Dep()
    hT = k.sb([128, NFF, TT], BF16, "hT")
    h_dep = [Dep() for _ in range(NFF)]
    hc1 = [(hT[:, c, :], h_dep[c]) for c in range(NFF)]
    f32r = Rot([k.sb([128, TT], F32, "f32r%d" % i) for i in range(8)])
    sa = f32r
    psr = Rot([k.ps([128, TT], F32, "psb%d" % i) for i in range(4)])
    psO = Rot([k.ps([128, TT], F32, "psO%d" % i) for i in range(2)])
    psM = k.ps([128, TT], F32, "psM")
    psM_d = Dep()
    psT = k.ps([128, 2 * TT], BF16, "psT")
    psT_d = Dep()
    WB = 5632
    wbuf = Rot([k.sb([128, WB], BF16, "wb%d" % i) for i in range(3)])

    def load_w(src, kc, n, eng="pool"):
        wt, wd = wbuf.get()
        v = wt[:, 0:kc * n].rearrange("p (a b) -> p a b", a=kc)
        op(eng, lambda e: e.dma_start(out=v, in_=src), writes=[wd], dma=True)
        return v, wd

    def mmacc(pt, pd, pairs, reads, m=None):
        n = len(pairs)
        for i, (lt, rh) in enumerate(pairs):
            op("pe", lambda e: e.matmul(pt, lt, rh, start=(i == 0), stop=(i == n - 1)),
               reads=reads, writes=[pd], inc=(i == n - 1))

    def rms_stats(nch, meanmat, n, sqc=None):
        if sqc is None:
            sqc = hc1
        pt, pd = psr.get()
        mmacc(pt[:, 0:n], pd, [(meanmat[:], sqc[c][0][:, 0:n]) for c in range(nch)], [sqc[c][1] for c in range(nch)] + [mean_dep])
        op("act", lambda e: e.activation(out=rstd[:, 0:n], in_=pt[:, 0:n], func=AF.Sqrt, bias=EPS),
           reads=[pd], writes=[rstd_dep])
        op("dve", lambda e: e.reciprocal(out=rstd[:, 0:n], in_=rstd[:, 0:n]), reads=[rstd_dep], writes=[rstd_dep])

    def rmsnorm(xl, g_ap_fn, gd, outl, n=TT, meanmat=None, sqc=None):
        if sqc is None:
            sqc = hc1
        nch = len(xl)
        for c, (xa, xdl) in enumerate(xl):
            op("act", lambda e: e.activation(out=sqc[c][0][:, 0:n], in_=xa, func=AF.Square), reads=xdl, writes=[sqc[c][1]])
        rms_stats(nch, meanmat if meanmat is not None else mean1024, n, sqc)
        for c, (xa, xdl) in enumerate(xl):
            oa, odl = outl[c]
            op("dve", lambda e: e.scalar_tensor_tensor(out=oa, in0=xa, scalar=g_ap_fn(c), in1=rstd[:, 0:n],
                                                       op0=ALU.mult, op1=ALU.mult),
               reads=xdl + [rstd_dep, gd], writes=odl)

    def xchunks(x, xdl):
        return [(x[:, c, :], xdl) for c in range(8)]

    def xnchunks():
        return [(xn[:, c, :], [xn_dep]) for c in range(8)]

    def ffn(l, which, tiles):
        for tl in tiles:
            rmsnorm(xchunks(tl["x"], tl["xdl"]), lambda c: gsb[which][:, l, c:c + 1], gdep[which], tl["xn"], sqc=tl["hc"])
        win = w_ffn_in[which][l].rearrange("(kc p) n -> p kc n", p=128)
        wout = w_ffn_out[which][l].rearrange("(kc p) n -> p kc n", p=128)
        groups = [(g * 512, 4) for g in range(5)] + [(2560, 2)]
        for c0, nchk in groups:
            va, wad = load_w(win[:, :, c0:c0 + nchk * 128], 8, nchk * 128)
            vb, wbd = load_w(win[:, :, DFF + c0:DFF + c0 + nchk * 128], 8, nchk * 128)
            for j in range(nchk):
                ff = c0 // 128 + j
                for tl in tiles:
                    xnl = tl["xn"]
                    xnd = [d_ for (_, dl_) in xnl for d_ in dl_]
                    pa, pad = psr.get()
                    pb, pbd = psr.get()
                    mmacc(pa[:], pad, [(va[:, c, j * 128:(j + 1) * 128], xnl[c][0]) for c in range(8)], [wad] + xnd)
                    mmacc(pb[:], pbd, [(vb[:, c, j * 128:(j + 1) * 128], xnl[c][0]) for c in range(8)], [wbd] + xnd)
                    st, sd = f32r.get()
                    op("act", lambda e: e.activation(out=st[:], in_=pa[:], func=AF.Silu), reads=[pad], writes=[sd])
                    ha, hd = tl["hc"][ff]
                    op("dve", lambda e: e.tensor_tensor(out=ha, in0=st[:], in1=pb[:], op=ALU.mult),
                       reads=[sd, pbd], writes=[hd])
        for m4 in range(2):
            v0, wd0 = load_w(wout[:, 0:11, m4 * 512:(m4 + 1) * 512], 11, 512)
            v1, wd1 = load_w(wout[:, 11:22, m4 * 512:(m4 + 1) * 512], 11, 512)
            for j in range(4):
                m = m4 * 4 + j
                for tl in tiles:
                    hc = tl["hc"]
                    po, pod = psr.get()
                    pairs = [((v0 if c < 11 else v1)[:, c % 11, j * 128:(j + 1) * 128], hc[c][0]) for c in range(NFF)]
                    mmacc(po[:], pod, pairs, [wd0, wd1] + [hc[c][1] for c in range(NFF)])
                    x = tl["x"]
                    op("dve", lambda e: e.scalar_tensor_tensor(out=x[:, m, :], in0=po[:], scalar=0.5, in1=x[:, m, :],
                                                               op0=ALU.mult, op1=ALU.add),
                       reads=[pod] + tl["xdl"], writes=tl["xdl"])

    cqn = k.sb([128, 3, TT], BF16, "cqn")
    ckvn = k.sb([128, 2, TT], BF16, "ckvn")
    cn_d = Dep()
    Qt = k.sb([128, 8, TT], BF16, "Qt")
    Kt = k.sb([128, 8, TT], BF16, "Kt")
    Vt = k.sb([128, 4, 8, 65], BF16, "Vt")
    Qt_d, Kt_d, Vt_d = Dep(), Dep(), Dep()
    op("pool", lambda e: e.memset(Vt[:], 1.0), writes=[Vt_d])
    rope = k.sb([96, 2, TT], F32, "rope")
    ropei = k.sb([96, TT], I32, "ropei")
    rope_d = Dep()
    wkr = k.sb([128, 8, 2, 96], BF16, "wkr")
    wkr_d = Dep()
    op("pool", lambda e: e.memset(wkr[:], 0.0), writes=[wkr_d])
    wuqs = k.sb([128, 3, 2, 768], BF16, "wuqs")
    wuq_d = Dep()
    op("pool", lambda e: e.memset(wuqs[:], 0.0), writes=[wuq_d])
    wukv = k.sb([128, 2, 2, 512], BF16, "wukv")
    wukv_d = Dep()
    khA = k.sb([128, 2, max(2 * T, 4096)], BF16, "khA")
    kh = Rot([khA[:, i, :] for i in range(2)])
    vh = Rot([k.sb([128, NB, 65], BF16, "vh%d" % i) for i in range(2)])
    Pt = Rot([k.sb([128, TT], BF16, "Pt%d" % i) for i in range(3)])
    ofull, ofull_d = hT[:, 8:12, :], h_dep[8:12]
    hgo, hgo_d = hT[:, 12:16, :], h_dep[12:16]
    omem, omem_d = hT[:, 16:20, :], h_dep[16:20]
    memn, memn_d = hT[:, 8:12, :].rearrange("p a (b c) -> p (a b) c", c=256), h_dep[8:12]
    mx = k.sb([128, 8, TT], BF16, "mx")
    mqT = mx[:, 0:4, :]
    mq_d = Dep()
    memK = k.sb([128, 4, 256], BF16, "memK")
    memV = k.sb([128, 2, 512], BF16, "memV")
    memKV_d = Dep()
    mergedb = k.sb([128, 8, TT], BF16, "mergedb")
    mgb_d = Dep()
    hgb = k.sb([128, TT], BF16, "hgb")
    hgb_d = Dep()
    vtok = mx[:, 4:8, :]
    vtok_d = Dep()
    qtl = k.sb([128, TT], BF16, "qtl")
    ktl = k.sb([128, TT], BF16, "ktl")
    khat = k.sb([128, TT], BF16, "khat")
    khtok = k.sb([128, 4, 128], BF16, "khtok")
    ATb = k.sb([128, 4, 128], BF16, "ATb")
    qtl_d, ktl_d, khat_d, khtok_d, ATb_d = Dep(), Dep(), Dep(), Dep(), Dep()
    S32 = k.sb([128, 4, 128], F32, "S32")
    Sbf = k.sb([128, 4, 128], BF16, "Sbf")
    S_d = [Dep() for _ in range(4)]
    Sb_d = [Dep() for _ in range(4)]

    xsv = xs.rearrange("(c p) t -> p c t", p=128)
    xinv = xT_in.rearrange("(c p) t -> p c t", p=128)
    yv = yT.rearrange("(c p) t -> p c t", p=128)
    memv = memT_in.rearrange("(c p) t -> p c t", p=128)

    def layer_setup(l):
        wi = w_in[l].rearrange("(kc p) n -> p kc n", p=128)
        op("pool", lambda e: e.dma_start(out=wkr[:, :, 0, 64:96], in_=wi[:, :, O_KR:O_KR + 32]), writes=[wkr_d], dma=True)
        op("pool", lambda e: e.dma_start(out=wkr[:, :, 1, 64:80], in_=wi[:, :, O_KR + 16:O_KR + 32]), writes=[wkr_d], dma=True)
        op("pool", lambda e: e.dma_start(out=wkr[:, :, 1, 80:96], in_=wi[:, :, O_KR:O_KR + 16]), writes=[wkr_d], dma=True)
        wq = w_uq[l].rearrange("(kc p) n -> p kc n", p=128)
        op("pool", lambda e: e.dma_start(out=wuqs[:, :, 0, :], in_=wq), writes=[wuq_d], dma=True)
        wq4 = w_uq[l].rearrange("(kc p) (h c) -> p kc h c", p=128, c=96)
        dst4 = wuqs[:, :, 1, :].rearrange("p k (h c) -> p k h c", c=96)
        for kc in range(3):
            op("pool", lambda e: e.dma_start(out=dst4[:, kc, :, 64:80], in_=wq4[:, kc, :, 80:96]), writes=[wuq_d], dma=True)
            op("pool", lambda e: e.dma_start(out=dst4[:, kc, :, 80:96], in_=wq4[:, kc, :, 64:80]), writes=[wuq_d], dma=True)
        op("pool", lambda e: e.dma_start(out=wukv[:, :, 0, :], in_=w_uk[l].rearrange("(kc p) n -> p kc n", p=128)),
           writes=[wukv_d], dma=True)
        op("pool", lambda e: e.dma_start(out=wukv[:, :, 1, :], in_=w_uv[l].rearrange("(kc p) n -> p kc n", p=128)),
           writes=[wukv_d], dma=True)
        ml = []
        for i in range(4):
            ft, fd = f32r.get()
            op("sp", lambda e: e.dma_start(out=ft[:].rearrange("p (a b) -> p a b", a=2), in_=memv[:, 2 * i:2 * i + 2, :]),
               writes=[fd], dma=True)
            ml += [(ft[:, 0:256], [fd]), (ft[:, 256:512], [fd])]
        rmsnorm(ml, lambda c: gmem[:, l, c:c + 1], gmem_d, [(memn[:, c, :], memn_d) for c in range(8)], n=256)
        wkv = w_mem_kv[l].rearrange("(kc p) n -> p kc n", p=128)
        vK, dK = load_w(wkv[:, :, 0:512], 8, 512)
        for h in range(4):
            pt, pd = psr.get()
            mmacc(pt[:, 0:256], pd, [(vK[:, c, h * 128:(h + 1) * 128], memn[:, c, :]) for c in range(8)], [dK] + memn_d)
            op("act", lambda e: e.activation(out=memK[:, h, :], in_=pt[:, 0:256], func=AF.Copy), reads=[pd], writes=[memKV_d])
        vV, dV = load_w(wkv[:, :, 512:1024], 8, 512)
        for mb in range(2):
            pt, pd = psr.get()
            mmacc(pt[:], pd, [(memn[:, c, mb * 128:(mb + 1) * 128], vV[:, c, :]) for c in range(8)], [dV] + memn_d)
            op("act", lambda e: e.activation(out=memV[:, mb, :], in_=pt[:], func=AF.Copy), reads=[pd], writes=[memKV_d])
        op("sp", lambda e: e.dma_start(out=S32[:], in_=Sg[0:512, :].rearrange("(h p) v -> p h v", p=128)),
           reads=[sg_d], writes=S_d, dma=True)
        op("dve", lambda e: e.tensor_scalar(out=S32[:], in0=S32[:], scalar1=flag[:, 0:1], scalar2=None, op0=ALU.mult),
           reads=S_d + [flag_d], writes=S_d)
        op("act", lambda e: e.activation(out=Sbf[:], in_=S32[:], func=AF.Copy), reads=S_d, writes=Sb_d)

    def rope_tables(t):
        R = slice(64, 96)
        scr = [f32r.get() for _ in range(4)]
        ang, tmp, nf, r = [b_[0][R, :] for b_ in scr]
        rd = [rope_d] + [b_[1] for b_ in scr]
        op("sp", lambda e: e.dma_start(out=ropei[R, :], in_=pos_in[t * TT:(t + 1) * TT].partition_broadcast(32)),
           writes=rd, dma=True)
        op("dve", lambda e: e.tensor_copy(out=tmp, in_=ropei[R, :]), reads=rd, writes=rd)
        op("dve", lambda e: e.tensor_scalar(out=ang, in0=tmp, scalar1=cst[R, 0:1], scalar2=None, op0=ALU.mult),
           reads=rd + [cst_d], writes=rd)
        for which in range(2):
            if which == 0:
                op("dve", lambda e: e.tensor_scalar(out=tmp, in0=ang, scalar1=float(np.pi / 2), scalar2=None, op0=ALU.add),
                   reads=rd, writes=rd)
                src = tmp
            else:
                src = ang
            op("dve", lambda e: e.tensor_scalar(out=ropei[R, :], in0=src, scalar1=float(1.0 / (2 * np.pi)), scalar2=None,
                                                op0=ALU.mult), reads=rd, writes=rd)
            op("dve", lambda e: e.tensor_copy(out=nf, in_=ropei[R, :]), reads=rd, writes=rd)
            op("dve", lambda e: e.scalar_tensor_tensor(out=r, in0=nf, scalar=-TWO_PI_HI, in1=src, op0=ALU.mult, op1=ALU.add),
               reads=rd, writes=rd)
            op("dve", lambda e: e.scalar_tensor_tensor(out=r, in0=nf, scalar=-TWO_PI_LO, in1=r, op0=ALU.mult, op1=ALU.add),
               reads=rd, writes=rd)
            op("dve", lambda e: e.tensor_scalar(out=r, in0=r, scalar1=-PI_SAFE, scalar2=PI_SAFE, op0=ALU.max, op1=ALU.min),
               reads=rd, writes=rd)
            op("act", lambda e: e.activation(out=rope[R, which, :], in_=r, func=AF.Sin), reads=rd, writes=rd)
        op("dve", lambda e: e.tensor_scalar(out=rope[R, 1, :], in0=rope[R, 1, :], scalar1=cst[R, 1:2], scalar2=None,
                                            op0=ALU.mult), reads=rd + [cst_d], writes=rd)

    def apply_rope(pA, pAd, pB, pBd, dst, dst_d):
        R = slice(64, 96)
        sc1, sc2 = f32r.get(), f32r.get()
        t1, t2 = sc1[0][R, :], sc2[0][R, :]
        op("dve", lambda e: e.tensor_tensor(out=t1, in0=pA[R, :], in1=rope[R, 0, :], op=ALU.mult),
           reads=[pAd, rope_d], writes=[sc1[1]])
        op("dve", lambda e: e.tensor_tensor(out=t2, in0=pB[R, :], in1=rope[R, 1, :], op=ALU.mult),
           reads=[pBd, rope_d], writes=[sc2[1]])
        op("dve", lambda e: e.tensor_tensor(out=dst, in0=t1, in1=t2, op=ALU.add), reads=[sc1[1], sc2[1]], writes=[dst_d])

    def mla_proj(l, t, wi):
        v, wd = load_w(wi[:, :, 0:640], 8, 640)
        cl = []
        for j in range(5):
            pt, pd = psr.get()
            mmacc(pt[:], pd, [(v[:, c, j * 128:(j + 1) * 128], xn[:, c, :]) for c in range(8)], [wd, xn_dep])
            ft, fd = f32r.get()
            op("act", lambda e: e.activation(out=ft[:], in_=pt[:], func=AF.Copy), reads=[pd], writes=[fd])
            cl.append((ft[:], [fd]))
        rmsnorm(cl[0:3], lambda c: gq[:, l, c:c + 1], gq_d, [(cqn[:, c, :], [cn_d]) for c in range(3)], meanmat=mean384)
        rmsnorm(cl[3:5], lambda c: gkv[:, l, c:c + 1], gkv_d, [(ckvn[:, c, :], [cn_d]) for c in range(2)], meanmat=mean256)
        op("sp", lambda e: e.dma_start(out=rope[64:96, :, :], in_=ropeD[t]), reads=[ropeD_d], writes=[rope_d], dma=True)
        pA, pAd = psr.get()
        pB, pBd = psr.get()
        mmacc(pA[0:96, :], pAd, [(wkr[:, c, 0, :], xn[:, c, :]) for c in range(8)], [wkr_d, xn_dep])
        mmacc(pB[0:96, :], pBd, [(wkr[:, c, 1, :], xn[:, c, :]) for c in range(8)], [wkr_d, xn_dep])
        apply_rope(pA, pAd, pB, pBd, Kt[64:96, 0, :], Kt_d)
        for h in range(1, 8):
            op("dve", lambda e: e.tensor_copy(out=Kt[64:96, h, :], in_=Kt[64:96, 0, :]), reads=[Kt_d], writes=[Kt_d])
        for h in range(8):
            pA, pAd = psr.get()
            pB, pBd = psr.get()
            mmacc(pA[0:96, :], pAd, [(wuqs[:, c, 0, h * 96:(h + 1) * 96], cqn[:, c, :]) for c in range(3)], [wuq_d, cn_d])
            mmacc(pB[0:96, :], pBd, [(wuqs[:, c, 1, h * 96:(h + 1) * 96], cqn[:, c, :]) for c in range(3)], [wuq_d, cn_d])
            op("act", lambda e: e.activation(out=Qt[0:64, h, :], in_=pA[0:64, :], func=AF.Copy), reads=[pAd], writes=[Qt_d])
            apply_rope(pA, pAd, pB, pBd, Qt[64:96, h, :], Qt_d)
        for hp in range(4):
            pt, pd = psr.get()
            mmacc(pt[:], pd, [(wukv[:, c, 0, hp * 128:(hp + 1) * 128], ckvn[:, c, :]) for c in range(2)], [wukv_d, cn_d])
            op("act", lambda e: e.activation(out=Kt[0:64, 2 * hp, :], in_=pt[0:64, :], func=AF.Copy), reads=[pd], writes=[Kt_d])
            op("act", lambda e: e.activation(out=Kt[0:64, 2 * hp + 1, :], in_=pt[64:128, :], func=AF.Copy), reads=[pd], writes=[Kt_d])
        for tb in range(4):
            pt, pd = psr.get()
            mmacc(pt[:], pd, [(ckvn[:, c, tb * 128:(tb + 1) * 128], wukv[:, c, 1, :]) for c in range(2)], [wukv_d, cn_d])
            op("act", lambda e: e.activation(out=Vt[:, tb, :, 0:64], in_=pt[:].rearrange("p (h c) -> p h c", c=64),
                                             func=AF.Copy), reads=[pd], writes=[Vt_d])
        par = l % 2
        op("sp", lambda e: e.dma_start(out=KTs2[par].rearrange("(h p) t -> p h t", p=96)[:, :, t * TT:(t + 1) * TT], in_=Kt[0:96, :, :]),
           reads=[Kt_d], writes=[kts_dep2[par][t]], dma=True)
        vs4 = Vs2[par].rearrange("(h p) (b c) -> h p b c", p=128, c=65)
        for h in range(8):
            op("sp", lambda e: e.dma_start(out=vs4[h, :, t * 4:(t + 1) * 4, :], in_=Vt[:, :, h, :]),
               reads=[Vt_d], writes=[vs_dep2[par][t]], dma=True)

    SCALE = float(96 ** -0.5)

    def mla_attn(l, t):
        par = l % 2
        nkb = PB + 4 * (t + 1)
        kts4 = KTs2[par].rearrange("(h p) t -> h p t", p=96)
        vs4 = Vs2[par].rearrange("(h p) (b c) -> h p b c", p=128, c=65)
        ktg4 = [g_.rearrange("(r h p) t -> r h p t", r=2, p=96) for g_ in KTg]
        vg5 = [g_.rearrange("(r h p) (b c) -> r h p b c", r=2, p=128, c=65) for g_ in Vg]
        pending = []

        def make_epi(h, po, pod):
            def epi():
                rrow, rrow_d = f32r.get()
                o32, o32_d = f32r.get()
                op("dve", lambda e: e.reciprocal(out=rrow[64:65, :], in_=po[64:65, :]), reads=[pod], writes=[rrow_d])
                op("pe", lambda e: e.matmul(psM[0:64, :], ones_f[64:65, :], rrow[64:65, :], start=True, stop=True),
                   reads=[rrow_d, mean_dep], writes=[psM_d])
                op("act", lambda e: e.activation(out=o32[0:64, :], in_=po[0:64, :], func=AF.Copy), reads=[pod], writes=[o32_d])
                r0 = (h % 2) * 64
                op("dve", lambda e: e.tensor_tensor(out=ofull[r0:r0 + 64, h // 2, :], in0=o32[0:64, :], in1=psM[0:64, :],
                                                    op=ALU.mult), reads=[o32_d, psM_d], writes=[ofull_d[h // 2]])
            return epi

        for h in range(8):
            kt, kd = kh.get()
            vt, vd = vh.get()
            op("sp", lambda e: e.dma_start(out=kt[0:96, 0:T], in_=ktg4[h // 2][0, h % 2, :, :]), reads=[kg_d[h // 2]], writes=[kd], dma=True)
            op("sp", lambda e: e.dma_start(out=kt[0:96, T:T + (t + 1) * TT], in_=kts4[h, :, 0:(t + 1) * TT]),
               reads=kts_dep2[par][0:t + 1], writes=[kd], dma=True)
            op("sp", lambda e: e.dma_start(out=vt[:, 0:PB, :], in_=vg5[h // 2][0, h % 2, :, :, :]), reads=[vg_d[h // 2]], writes=[vd], dma=True)
            op("sp", lambda e: e.dma_start(out=vt[:, PB:nkb, :], in_=vs4[h, :, 0:4 * (t + 1), :]),
               reads=vs_dep2[par][0:t + 1], writes=[vd], dma=True)
            op("dve", lambda e: e.tensor_scalar(out=vt[:, 0:PB, :], in0=vt[:, 0:PB, :], scalar1=flag[:, 0:1], scalar2=None,
                                                op0=ALU.mult), reads=[vd, flag_d], writes=[vd])
            po, pod = psO.get()

            def q0_of(kb):
                return max(0, kb - PB - 4 * t) * 128

            def emit_S(kb):
                ps_, psd = psr.get()
                q0 = q0_of(kb)
                op("pe", lambda e: e.matmul(ps_[:, q0:TT], kt[0:96, kb * 128:(kb + 1) * 128], Qt[0:96, h, q0:TT],
                                            start=True, stop=True), reads=[kd, Qt_d], writes=[psd])
                return ps_, psd

            nxt = emit_S(0)
            for kb in range(nkb):
                if kb == 3 and pending:
                    pending.pop(0)()
                ps_, psd = nxt
                if kb + 1 < nkb:
                    nxt = emit_S(kb + 1)
                q0 = q0_of(kb)
                p_, p_d = Pt.get()
                op("act", lambda e: e.activation(out=p_[:, q0:TT], in_=ps_[:, q0:TT], func=AF.Exp, scale=SCALE),
                   reads=[psd], writes=[p_d])
                if kb >= PB + 4 * t:
                    op("dve", lambda e: e.tensor_tensor(out=p_[:, q0:q0 + 128], in0=p_[:, q0:q0 + 128], in1=tri[:],
                                                         op=ALU.mult), reads=[p_d, tri_d], writes=[p_d])
                op("pe", lambda e: e.matmul(po[0:65, q0:TT], vt[:, kb, :], p_[:, q0:TT], start=(kb == 0),
                                            stop=(kb == nkb - 1)), reads=[vd, p_d], writes=[pod], inc=True)
            pending.append(make_epi(h, po, pod))
        while pending:
            pending.pop(0)()

    MSCALE = float(128 ** -0.5)

    def mem_attn(l, wi):
        v, wd = load_w(wi[:, :, O_MQ:O_MQ + 512], 8, 512)
        for h in range(4):
            pt, pd = psr.get()
            mmacc(pt[:], pd, [(v[:, c, h * 128:(h + 1) * 128], xn[:, c, :]) for c in range(8)], [wd, xn_dep])
            op("act", lambda e: e.activation(out=mqT[:, h, :], in_=pt[:], func=AF.Copy), reads=[pd], writes=[mq_d])
        for h in range(4):
            po, pod = psO.get()
            pden, pdend = psO.get()
            ps_list = []
            for mb in range(2):
                ps_, psd = psr.get()
                op("pe", lambda e: e.matmul(ps_[:], memK[:, h, mb * 128:(mb + 1) * 128], mqT[:, h, :], start=True, stop=True),
                   reads=[memKV_d, mq_d], writes=[psd])
                p_, p_d = Pt.get()
                op("act", lambda e: e.activation(out=p_[:], in_=ps_[:], func=AF.Exp, scale=MSCALE), reads=[psd], writes=[p_d])
                ps_list.append((p_, p_d))
            for mb in range(2):
                p_, p_d = ps_list[mb]
                op("pe", lambda e: e.matmul(po[:], memV[:, mb, h * 128:(h + 1) * 128], p_[:], start=(mb == 0), stop=(mb == 1)),
                   reads=[memKV_d, p_d], writes=[pod])
            for mb in range(2):
                p_, p_d = ps_list[mb]
                op("pe", lambda e: e.matmul(pden[:], ones_bf[:], p_[:], start=(mb == 0), stop=(mb == 1)),
                   reads=[mean_dep, p_d], writes=[pdend])
            st, sd = f32r.get()
            op("dve", lambda e: e.reciprocal(out=st[:], in_=pden[:]), reads=[pdend], writes=[sd])
            op("dve", lambda e: e.tensor_tensor(out=omem[:, h, :], in0=st[:], in1=po[:], op=ALU.mult),
               reads=[sd, pod], writes=[omem_d[h]])

    def hgrn(l, t, wi):
        v, wd = load_w(wi[:, :, O_HI:O_HI + 512], 8, 512)
        for tb in range(4):
            pt, pd = psr.get()
            mmacc(pt[:], pd, [(xn[:, c, tb * 128:(tb + 1) * 128], v[:, c, :]) for c in range(8)], [wd, xn_dep])
            op("act", lambda e: e.activation(out=vtok[:, tb, :], in_=pt[:], func=AF.Copy), reads=[pd], writes=[vtok_d])
        sets = [dict(qtl=qtl[:], qd=qtl_d, ktl=ktl[:], kd=ktl_d, khat=khat[:], khd=khat_d, hgb=hgb[:], hd=hgb_d),
                dict(qtl=hT[:, 1, :], qd=h_dep[1], ktl=hT[:, 2, :], kd=h_dep[2], khat=hT[:, 3, :], khd=h_dep[3],
                     hgb=hT[:, 4, :], hd=h_dep[4])]

        def chain(h):
            B = sets[h % 2]
            wt, wd = wbuf.get()
            v = wt[:, 0:8 * 384].rearrange("p (a b) -> p a b", a=8)
            for i, off in enumerate((O_HQ, O_HF, O_HG)):
                op("pool", lambda e: e.dma_start(out=v[:, :, i * 128:(i + 1) * 128],
                                                 in_=wi[:, :, off + h * 128:off + (h + 1) * 128]), writes=[wd], dma=True)
            q_, q_d = f32r.get()
            f_, f_d = f32r.get()
            g_, g_d = f32r.get()
            G_, G_d = f32r.get()
            E_, E_d = f32r.get()
            for i, (dst, dd, fn) in enumerate(((q_[:], q_d, AF.Silu), (f_[:], f_d, AF.Sigmoid), (B["hgb"], B["hd"], AF.Silu))):
                pt, pd = psr.get()
                mmacc(pt[:], pd, [(v[:, c, i * 128:(i + 1) * 128], xn[:, c, :]) for c in range(8)], [wd, xn_dep])
                op("act", lambda e: e.activation(out=dst, in_=pt[:], func=fn), reads=[pd], writes=[dd])
            op("dve", lambda e: e.tensor_scalar(out=f_[:], in0=f_[:], scalar1=oml[:, l, h:h + 1], scalar2=lbs[:, l, h:h + 1],
                                                op0=ALU.mult, op1=ALU.add), reads=[f_d, lb_d], writes=[f_d])
            op("act", lambda e: e.activation(out=g_[:], in_=f_[:], func=AF.Ln), reads=[f_d], writes=[g_d])
            op("dve", lambda e: e.tensor_tensor_scan(out=G_[:], data0=rst[:], data1=g_[:], initial=0.0, op0=ALU.mult, op1=ALU.add),
               reads=[g_d, rst_d], writes=[G_d])
            op("act", lambda e: e.activation(out=E_[:], in_=G_[:], func=AF.Exp), reads=[G_d], writes=[E_d])
            op("dve", lambda e: e.tensor_tensor(out=B["qtl"], in0=q_[:], in1=E_[:], op=ALU.mult),
               reads=[q_d, E_d], writes=[B["qd"]])
            op("act", lambda e: e.activation(out=g_[:], in_=G_[:], func=AF.Exp, scale=-1.0), reads=[G_d, g_d], writes=[g_d])
            op("dve", lambda e: e.tensor_scalar(out=f_[:], in0=f_[:], scalar1=-1.0, scalar2=1.0, op0=ALU.mult, op1=ALU.add),
               reads=[f_d], writes=[f_d])
            k32, k32_d = q_, q_d
            op("dve", lambda e: e.tensor_tensor(out=k32[:], in0=f_[:], in1=g_[:], op=ALU.mult), reads=[f_d, g_d], writes=[k32_d])
            op("act", lambda e: e.activation(out=B["ktl"], in_=k32[:], func=AF.Copy), reads=[k32_d], writes=[B["kd"]])
            for ci in range(8):
                cs = slice(ci * 64, (ci + 1) * 64)
                op("dve", lambda e: e.tensor_scalar(out=B["khat"][:, cs], in0=k32[:, cs], scalar1=E_[:, ci * 64 + 63:ci * 64 + 64],
                                                    scalar2=None, op0=ALU.mult), reads=[k32_d, E_d], writes=[B["khd"]])
            return dict(B=B, E_=E_, E_d=E_d)

        kt2 = [(khtok, khtok_d), (hT[:, 5, :].rearrange("p (a b) -> p a b", a=4), h_dep[5])]
        at2 = [(ATb, ATb_d), (hT[:, 6, :].rearrange("p (a b) -> p a b", a=4), h_dep[6])]

        def rest_gen(h, st_):
            B, E_, E_d = st_["B"], st_["E_"], st_["E_d"]
            khtok_, khtok_d_ = kt2[h % 2]
            ATb_, ATb_d_ = at2[h % 2]
            for tb in range(4):
                op("pe", lambda e: e.transpose(psT[:, tb * 128:(tb + 1) * 128], B["khat"][:, tb * 128:(tb + 1) * 128], ident[:]),
                   reads=[B["khd"], ident_d], writes=[psT_d], inc=(tb == 3))
            op("act", lambda e: e.activation(out=khtok_[:].rearrange("p a b -> p (a b)") if h % 2 == 0 else hT[:, 5, :],
                                             in_=psT[:, 0:512], func=AF.Copy), reads=[psT_d], writes=[khtok_d_])
            pat, patd = psr.get()
            for tb in range(4):
                bs = slice(tb * 128, (tb + 1) * 128)
                op("pe", lambda e: e.matmul(pat[:, bs], B["ktl"][:, bs], B["qtl"][:, bs], start=True, stop=True),
                   reads=[B["kd"], B["qd"]], writes=[patd], inc=(tb == 3))
            for tb in range(4):
                bs = slice(tb * 128, (tb + 1) * 128)
                op("dve", lambda e: e.tensor_tensor(out=ATb_[:, tb, :], in0=pat[:, bs], in1=blk[:], op=ALU.mult),
                   reads=[patd, blk_d], writes=[ATb_d_])
            po, pod = psO.get()
            yield
            for tb in range(4):
                bs = slice(tb * 128, (tb + 1) * 128)
                op("pe", lambda e: e.matmul(po[:, bs], vtok[:, tb, h * 128:(h + 1) * 128], ATb_[:, tb, :], start=True, stop=False),
                   reads=[vtok_d, ATb_d_], writes=[pod], inc=False)
                for half in range(2):
                    ci = tb * 2 + half
                    cs = slice(ci * 64, (ci + 1) * 64)
                    rs = slice(half * 64, half * 64 + 64)
                    last = (half == 1)
                    op("pe", lambda e: e.matmul(po[:, cs], Sbf[:, h, :], B["qtl"][:, cs], start=False, stop=last),
                       reads=[Sb_d[h], B["qd"]], writes=[pod], inc=True)
                    pu, pud = psr.get()
                    op("pe", lambda e: e.matmul(pu[:, 0:128], khtok_[rs, tb, :], vtok[rs, tb, h * 128:(h + 1) * 128],
                                                start=True, stop=True), reads=[khtok_d_, vtok_d], writes=[pud])
                    op("dve", lambda e: e.scalar_tensor_tensor(out=S32[:, h, :], in0=S32[:, h, :],
                                                               scalar=E_[:, ci * 64 + 63:ci * 64 + 64], in1=pu[:, 0:128],
                                                               op0=ALU.mult, op1=ALU.add),
                       reads=[S_d[h], E_d, pud], writes=[S_d[h]])
                    op("act", lambda e: e.activation(out=Sbf[:, h, :], in_=S32[:, h, :], func=AF.Copy), reads=[S_d[h]], writes=[Sb_d[h]])
                    yield
            o32, o32_d = f32r.get()
            st, sd = f32r.get()
            op("act", lambda e: e.activation(out=o32[:], in_=po[:], func=AF.Copy), reads=[pod], writes=[o32_d])
            op("act", lambda e: e.activation(out=hT[:, 0, :], in_=po[:], func=AF.Square), reads=[pod], writes=[h_dep[0]])
            rms_stats(1, mean128, TT)
            op("dve", lambda e: e.scalar_tensor_tensor(out=st[:], in0=o32[:], scalar=ghg[:, l:l + 1], in1=rstd[:],
                                                       op0=ALU.mult, op1=ALU.mult), reads=[o32_d, rstd_dep, ghg_d], writes=[sd])
            op("dve", lambda e: e.tensor_tensor(out=hgo[:, h, :], in0=st[:], in1=B["hgb"], op=ALU.mult),
               reads=[sd, B["hd"]], writes=[hgo_d[h]])

        for hp in range(2):
            h0, h1 = 2 * hp, 2 * hp + 1
            st0 = chain(h0)
            st1 = chain(h1)
            g0, g1 = rest_gen(h0, st0), rest_gen(h1, st1)
            next(g0)
            next(g1)
            alive = [g0, g1]
            done_tail = []
            while alive:
                for g_ in list(alive):
                    try:
                        next(g_)
                    except StopIteration:
                        alive.remove(g_)

    def merge_out(l, x, xdep, wi):
        branches = ((0, ofull, ofull_d), (1, hgo, hgo_d), (2, omem, omem_d))
        for b, ob, obd in branches:
            if not cfg.get("br%d" % b, True):
                continue
            wo_v = w_o[b][l].rearrange("(kc p) n -> p kc n", p=128)
            for half in range(2):
                vo, wod = load_w(wo_v[:, :, half * 512:(half + 1) * 512], 4, 512)
                vg, wgd = load_w(wi[:, :, O_GATE + b * 1024 + half * 512:O_GATE + b * 1024 + (half + 1) * 512], 8, 512)
                for j in range(4):
                    m = half * 4 + j
                    py, pyd = psr.get()
                    pg, pgd = psr.get()
                    mmacc(py[:], pyd, [(vo[:, c, j * 128:(j + 1) * 128], ob[:, c, :]) for c in range(4)], [wod] + obd)
                    mmacc(pg[:], pgd, [(vg[:, c, j * 128:(j + 1) * 128], xn[:, c, :]) for c in range(8)], [wgd, xn_dep])
                    st, sd = f32r.get()
                    op("act", lambda e: e.activation(out=st[:], in_=pg[:], func=AF.Sigmoid), reads=[pgd], writes=[sd])
                    if b == 0:
                        op("dve", lambda e: e.tensor_tensor(out=mergedb[:, m, :], in0=st[:], in1=py[:], op=ALU.mult),
                           reads=[sd, pyd], writes=[mgb_d])
                    else:
                        op("dve", lambda e: e.tensor_tensor(out=st[:], in0=st[:], in1=py[:], op=ALU.mult),
                           reads=[sd, pyd], writes=[sd])
                        op("dve", lambda e: e.tensor_tensor(out=mergedb[:, m, :], in0=mergedb[:, m, :], in1=st[:], op=ALU.add),
                           reads=[sd, mgb_d], writes=[mgb_d])
        wov = w_out[l].rearrange("(kc p) n -> p kc n", p=128)
        for half in range(2):
            v, wd = load_w(wov[:, :, half * 512:(half + 1) * 512], 8, 512)
            for j in range(4):
                m = half * 4 + j
                pt, pd = psr.get()
                mmacc(pt[:], pd, [(v[:, c, j * 128:(j + 1) * 128], mergedb[:, c, :]) for c in range(8)], [wd, mgb_d])
                op("dve", lambda e: e.tensor_tensor(out=x[:, m, :], in0=x[:, m, :], in1=pt[:], op=ALU.add),
                   reads=[pd, xdep], writes=[xdep])

    def mixer(l, t, x, xdep):
        rmsnorm(xchunks(x, [xdep]), lambda c: gmix[:, l, c:c + 1], gmix_d, xnchunks())
        wi = w_in[l].rearrange("(kc p) n -> p kc n", p=128)
        if cfg.get("br0", True):
            mla_proj(l, t, wi)
            mla_attn(l, t)
        if cfg.get("br1", True):
            hgrn(l, t, wi)
        if cfg.get("br2", True):
            mem_attn(l, wi)
        merge_out(l, x, xdep, wi)

    for t in range(NT):
        rope_tables(t)
        op("sp", lambda e: e.dma_start(out=ropeD[t], in_=rope[64:96, :, :]), reads=[rope_d], writes=[ropeD_d], dma=True)
    zt, zd = hT[:, 0:4, :].rearrange("p a b -> p (a b)"), h_dep[0:4]
    op("dve", lambda e: e.memset(zt, 0.0), writes=zd)
    zf, zfd = f32r.get()
    op("dve", lambda e: e.memset(zf[:], 0.0), writes=[zfd])
    for pc in range(4):
        for i in range(3):
            op("sp", lambda e: e.dma_start(out=KTg[pc][i * 128:(i + 1) * 128, :], in_=zt[:, 0:T]), reads=zd, writes=[kg_d[pc]], dma=True)
        for i in range(4):
            op("sp", lambda e: e.dma_start(out=Vg[pc][i * 128:(i + 1) * 128, :], in_=zt[:, 0:PB * 65]), reads=zd, writes=[vg_d[pc]], dma=True)
    for i in range(8):
        op("sp", lambda e: e.dma_start(out=Sg[i * 128:(i + 1) * 128, :], in_=zf[:, 0:128]), reads=[zfd], writes=[sg_d], dma=True)

    def exchange(l):
        par = l % 2
        op("sp", lambda e: e.dma_start(out=Ss.rearrange("(h p) v -> p h v", p=128), in_=S32[:]), reads=S_d, writes=[ss_d], dma=True)
        for pc in range(4):
            op("pool", lambda e: e.collective_compute("AllGather", ALU.bypass, replica_groups=RG,
                                                      ins=[KTs2[par][pc * 192:(pc + 1) * 192, :]], outs=[KTg[pc]]),
               reads=kts_dep2[par], writes=[kg_d[pc]], dma=True, key=("ccK%d" % pc, "c"))
            op("pool", lambda e: e.collective_compute("AllGather", ALU.bypass, replica_groups=RG,
                                                      ins=[Vs2[par][pc * 256:(pc + 1) * 256, :]], outs=[Vg[pc]]),
               reads=vs_dep2[par], writes=[vg_d[pc]], dma=True, key=("ccV%d" % pc, "c"))
        op("pool", lambda e: e.collective_compute("AllGather", ALU.bypass, replica_groups=RG, ins=[Ss],
                                                  outs=[Sg]),
           reads=[ss_d], writes=[sg_d], dma=True, key=("ccS", "c"))

    khdeps = [kh.t[0][1], kh.t[1][1]]
    x_b = khA[:].rearrange("p a b -> p (a b)").bitcast(F32).rearrange("p (c t) -> p c t", c=8)
    xn_b = [(mx[:, c, :], [mq_d if c < 4 else vtok_d]) for c in range(8)]
    hc2 = [(Qt[:, c, :], Qt_d) for c in range(8)] + [(Kt[:, c, :], Kt_d) for c in range(8)] + \
          [(mergedb[:, c, :], mgb_d) for c in range(6)]
    tile_a = dict(x=xtile[:], xdl=[xd], xn=xnchunks(), hc=hc1)
    tile_b = dict(x=x_b, xdl=khdeps, xn=xn_b, hc=hc2)

    def ffn_phase(l_prev, l_next, final_out):
        for p in range(NT // 2):
            tls = [tile_a, tile_b]
            for i, tl in enumerate(tls):
                t = 2 * p + i
                src = xinv if (l_prev is None) else xsv
                op("sp", lambda e: e.dma_start(out=tl["x"], in_=src[:, :, t * TT:(t + 1) * TT]),
                   reads=[xs_dep[t]], writes=tl["xdl"], dma=True)
            if l_prev is not None:
                ffn(l_prev, 1, tls)
            if l_next is not None:
                ffn(l_next, 0, tls)
            for i, tl in enumerate(tls):
                t = 2 * p + i
                if final_out:
                    rmsnorm(xchunks(tl["x"], tl["xdl"]), lambda c: gfin[:, c:c + 1], gfin_dep, xchunks(tl["x"], tl["xdl"]),
                            sqc=tl["hc"])
                    op("sp", lambda e: e.dma_start(out=yv[:, :, t * TT:(t + 1) * TT], in_=tl["x"]),
                       reads=tl["xdl"], writes=[y_dep[t]], dma=True)
                else:
                    op("sp", lambda e: e.dma_start(out=xsv[:, :, t * TT:(t + 1) * TT], in_=tl["x"]),
                       reads=tl["xdl"], writes=[xs_dep[t]], dma=True)

    for l in range(depth):
        ffn_phase(None if l == 0 else l - 1, l, False)
        layer_setup(l)
        for t in range(NT):
            x = xtile
            op("sp", lambda e: e.dma_start(out=x[:], in_=xsv[:, :, t * TT:(t + 1) * TT]),
               reads=[xs_dep[t]], writes=[xd], dma=True)
            mixer(l, t, x, xd)
            op("sp", lambda e: e.dma_start(out=xsv[:, :, t * TT:(t + 1) * TT], in_=x[:]),
               reads=[xd], writes=[xs_dep[t]], dma=True)
        if l < depth - 1:
            exchange(l)
    ffn_phase(depth - 1, None, True)
    k.finish(y_dep)
    return nc, es


def host_inputs(inp, b, half, zc, depth=4, TL=2048):
    f = np.float32
    NS = depth + 1

    def stack(a):
        a = np.asarray(a, f)
        key = (id(a), half)
        z = np.zeros((1,) + a.shape[1:], f)
        return np.concatenate([a, z], 0) if half == 0 else np.concatenate([z, a], 0)

    def fm(g, n):
        return np.ascontiguousarray(stack(g).reshape(NS, n, 128).transpose(2, 0, 1))

    consts = np.zeros((128, 8), f)
    inv_freq = (np.float32(10000.0) ** (-np.arange(0, 32, 2, dtype=np.float32) / np.float32(32))).astype(f)
    consts[64:80, 0] = inv_freq
    consts[80:96, 0] = inv_freq
    consts[64:80, 1] = -1.0
    consts[80:96, 1] = 1.0
    idx = np.arange(128)
    tri = (idx[None, :] >= idx[:, None]).astype(f)
    blk = tri * ((idx[None, :] // 64) == (idx[:, None] // 64))
    rst = np.ones((128, TT), f)
    rst[:, ::64] = 0.0
    onehot = np.zeros((128, NS, depth, 4), f)
    for l in range(depth):
        onehot[:, l + half, l, :] = 1.0
    sl = slice(half * TL, (half + 1) * TL)
    m = dict(
        xT=np.ascontiguousarray(np.asarray(inp["x"][b], f)[sl].T),
        memT=np.ascontiguousarray(np.asarray(inp["mem"][b], f).T),
        pos=np.ascontiguousarray(np.asarray(inp["positions"][b], np.int32)[sl]),
        flag=np.full((128, 1), float(half), f),
        g_ffn1=fm(inp["ffn1_norm"], 8), g_ffn2=fm(inp["ffn2_norm"], 8), g_mix=fm(inp["mix_norm"], 8),
        g_mem=fm(inp["mem_norm"], 8), g_q=fm(inp["q_lat_norm"], 3), g_kv=fm(inp["kv_lat_norm"], 2),
        g_hg=np.ascontiguousarray(stack(inp["hg_out_norm"]).T),
        hlb=np.ascontiguousarray(np.asarray(inp["hg_lower_bounds"], f).reshape(depth, 4, 128).transpose(2, 0, 1)),
        lsel=onehot,
        g_final=np.ascontiguousarray(np.asarray(inp["final_norm"], f).reshape(8, 128).T),
        c_consts=consts, c_tri=tri, c_blk=blk.astype(f), c_rst=rst, c_ident=np.eye(128, dtype=f),
    )
    for nm in ("w_ffn1_in", "w_ffn1_out", "w_ffn2_in", "w_ffn2_out", "w_in", "w_uq", "w_uk", "w_uv", "w_o_mla",
               "w_o_hg", "w_o_mem", "w_mem_kv", "w_out"):
        if (nm, half) not in zc:
            zc[(nm, half)] = stack(inp[nm])
        m[nm] = zc[(nm, half)]
    return m


def kernel(**inp):
    B, S, depth = 4, 4096, 4
    TL = S // 2
    nc, es = build(dict(T=TL, depth=depth + 1, dtot=depth))
    zc = {}
    maps = [host_inputs(inp, core // 2, core % 2, zc) for core in range(8)]
    res = run_bass_kernel_spmd(nc, maps, core_ids=list(range(8)))
    out = np.empty((B, S, D), np.float32)
    for core in range(8):
        b, half = core // 2, core % 2
        out[b, half * TL:(half + 1) * TL, :] = res.results[core]["yT"].T
    return out
```
